# Optimizing a Trainium2 kernel written in Bass

```python
import jax, jax.numpy as jnp
from jax import lax
import numpy as np

D_MODEL = 1024
BATCH = 8
SEQ = 2048
DEPTH = 2
DEC_BATCH = 128
DEC_SEQ = 8
PAST_LEN = 16384
PAGE_SIZE = 128

N_MIXERS = 2
HGRN_EXPAND = 128
HGRN_HEADS = D_MODEL // HGRN_EXPAND
HGRN_DK = HGRN_EXPAND
HGRN_DV = D_MODEL // HGRN_HEADS
HGRN_FDIM = HGRN_HEADS * HGRN_DK
HGRN_VDIM = HGRN_HEADS * HGRN_DV
HGRN_CHUNK = 32
CONV_W = 3
D_FF = 2816
FFN_RES = 0.5
NORM_EPS = 1e-6
N_HGRN = (DEPTH + 1) // 2
N_CONV = DEPTH // 2

kernel_name = "hgrn2_shortconv_macaron_decode_step"


def rms_norm(x, gain):
    xf = x.astype(jnp.float32)
    r = lax.rsqrt(jnp.mean(xf * xf, axis=-1, keepdims=True) + NORM_EPS)
    return (xf * r).astype(x.dtype) * gain


def swiglu_ffn(x, w_up, w_down):
    gate, up = jnp.split(x @ w_up, 2, axis=-1)
    return (jax.nn.silu(gate) * up) @ w_down


def hgrn2_recurrence(q, k, v, logf, s0):
    B, T, H, _ = q.shape
    chunk = min(HGRN_CHUNK, T)
    n_chunks = -(-T // chunk)
    pad = n_chunks * chunk - T

    def to_chunks(a):
        a = jnp.pad(a, ((0, 0), (0, pad), (0, 0), (0, 0)))
        return jnp.moveaxis(a.reshape(B, n_chunks, chunk, H, a.shape[-1]), 1, 0)

    causal = jnp.tril(jnp.ones((chunk, chunk), dtype=bool))[None, :, :, None, None]

    def step(S, blk):
        qc, kc, vc, gc = blk
        b = jnp.cumsum(gc, axis=1)
        inter = jnp.einsum('bthk,bhkv->bthv', qc * jnp.exp(b), S)
        decay = jnp.exp(jnp.where(causal, b[:, :, None] - b[:, None, :], -jnp.inf))
        scores = jnp.sum(qc[:, :, None] * kc[:, None, :] * decay, axis=-1)
        intra = jnp.einsum('btsh,bshv->bthv', scores, vc)
        b_last = b[:, -1]
        S = jnp.exp(b_last)[..., None] * S + jnp.einsum(
            'bshk,bshv->bhkv', kc * jnp.exp(b_last[:, None] - b), vc)
        return S, inter + intra

    s_final, o = lax.scan(step, s0, (to_chunks(q), to_chunks(k), to_chunks(v), to_chunks(logf)))
    o = jnp.moveaxis(o, 0, 1).reshape(B, n_chunks * chunk, H, -1)[:, :T]
    return o, s_final


def hgrn2_mixer(x, s0, w_in, lower_bound, norm_gain, w_out):
    B, T, _ = x.shape
    proj = x @ w_in
    q, f, i, g = jnp.split(proj, [HGRN_FDIM, 2 * HGRN_FDIM, 2 * HGRN_FDIM + HGRN_VDIM], axis=-1)
    q = jax.nn.silu(q.astype(jnp.float32))
    fgate = lower_bound + (1.0 - lower_bound) * jax.nn.sigmoid(f.astype(jnp.float32))
    logf = jnp.log(fgate)
    k = 1.0 - fgate
    o, s_new = hgrn2_recurrence(
        q.reshape(B, T, HGRN_HEADS, HGRN_DK),
        k.reshape(B, T, HGRN_HEADS, HGRN_DK),
        i.astype(jnp.float32).reshape(B, T, HGRN_HEADS, HGRN_DV),
        logf.reshape(B, T, HGRN_HEADS, HGRN_DK),
        s0.astype(jnp.float32))
    o = rms_norm(o, norm_gain.reshape(HGRN_HEADS, HGRN_DV))
    o = o.reshape(B, T, HGRN_VDIM).astype(x.dtype) * jax.nn.sigmoid(g)
    return o @ w_out, s_new.astype(s0.dtype)


def shortconv_mixer(x, buf, w_in, conv_w, w_out):
    T = x.shape[1]
    b_gate, c_gate, v = jnp.split(x @ w_in, 3, axis=-1)
    u = c_gate * v
    ext = jnp.concatenate([buf.astype(u.dtype), u], axis=1)
    conv = sum(conv_w[tap] * ext[:, tap:tap + T] for tap in range(CONV_W))
    y = (b_gate * conv) @ w_out
    return y, ext[:, -(CONV_W - 1):]


def decoder_trunk(x, state_hgrn, state_conv, norm_ffn1, w_ffn1_up, w_ffn1_down, norm_mix,
                  norm_ffn2, w_ffn2_up, w_ffn2_down, w_hgrn_in, lower_bounds, hgrn_norm,
                  w_hgrn_out, w_conv_in, conv_w, w_conv_out, norm_final):
    new_hgrn, new_conv = [], []
    for layer in range(DEPTH):
        x = x + FFN_RES * swiglu_ffn(rms_norm(x, norm_ffn1[layer]), w_ffn1_up[layer], w_ffn1_down[layer])
        xn = rms_norm(x, norm_mix[layer])
        j = layer // N_MIXERS
        if layer % N_MIXERS == 0:
            mix, s = hgrn2_mixer(xn, state_hgrn[j], w_hgrn_in[j], lower_bounds[j], hgrn_norm[j], w_hgrn_out[j])
            new_hgrn.append(s)
        else:
            mix, s = shortconv_mixer(xn, state_conv[j], w_conv_in[j], conv_w[j], w_conv_out[j])
            new_conv.append(s)
        x = x + mix
        x = x + FFN_RES * swiglu_ffn(rms_norm(x, norm_ffn2[layer]), w_ffn2_up[layer], w_ffn2_down[layer])
    return rms_norm(x, norm_final), jnp.stack(new_hgrn), jnp.stack(new_conv)


def setup_inputs(seed: int = 0) -> dict:
    key = jax.random.key(seed)
    ks = jax.random.split(key, 20)

    def nrm(k, shape, scale):
        return jax.random.normal(k, shape, jnp.float32) * scale

    def gain(k, shape):
        return 1.0 + 0.01 * jax.random.normal(k, shape, jnp.float32)

    return {
        "x_prompt": nrm(ks[0], (BATCH, SEQ, D_MODEL), 1.0),
        "x_sample": nrm(ks[1], (DEC_BATCH, DEC_SEQ, D_MODEL), 1.0),
        "state_hgrn": nrm(ks[2], (N_HGRN, DEC_BATCH, HGRN_HEADS, HGRN_DK, HGRN_DV), 0.5),
        "state_conv": nrm(ks[3], (N_CONV, DEC_BATCH, CONV_W - 1, D_MODEL), 1.0),
        "norm_ffn1": gain(ks[4], (DEPTH, D_MODEL)),
        "w_ffn1_up": nrm(ks[5], (DEPTH, D_MODEL, 2 * D_FF), D_MODEL ** -0.5),
        "w_ffn1_down": nrm(ks[6], (DEPTH, D_FF, D_MODEL), D_FF ** -0.5),
        "norm_mix": gain(ks[7], (DEPTH, D_MODEL)),
        "norm_ffn2": gain(ks[8], (DEPTH, D_MODEL)),
        "w_ffn2_up": nrm(ks[9], (DEPTH, D_MODEL, 2 * D_FF), D_MODEL ** -0.5),
        "w_ffn2_down": nrm(ks[10], (DEPTH, D_FF, D_MODEL), D_FF ** -0.5),
        "w_hgrn_in": nrm(ks[11], (N_HGRN, D_MODEL, 2 * HGRN_FDIM + 2 * HGRN_VDIM), D_MODEL ** -0.5),
        "hgrn_lower_bounds": nrm(ks[12], (N_HGRN + 1, HGRN_FDIM), 0.1),
        "hgrn_norm": gain(ks[13], (N_HGRN, HGRN_VDIM)),
        "w_hgrn_out": nrm(ks[14], (N_HGRN, HGRN_VDIM, D_MODEL), HGRN_VDIM ** -0.5),
        "w_conv_in": nrm(ks[15], (N_CONV, D_MODEL, 3 * D_MODEL), D_MODEL ** -0.5),
        "conv_w": nrm(ks[16], (N_CONV, CONV_W, D_MODEL), 0.5),
        "w_conv_out": nrm(ks[17], (N_CONV, D_MODEL, D_MODEL), D_MODEL ** -0.5),
        "norm_final": gain(ks[18], (D_MODEL,)),
    }


def reference(x_prompt, x_sample, state_hgrn, state_conv, norm_ffn1, w_ffn1_up, w_ffn1_down,
              norm_mix, norm_ffn2, w_ffn2_up, w_ffn2_down, w_hgrn_in, hgrn_lower_bounds,
              hgrn_norm, w_hgrn_out, w_conv_in, conv_w, w_conv_out, norm_final):
    lower_bounds = jnp.cumsum(jax.nn.softmax(hgrn_lower_bounds.astype(jnp.float32), axis=0), axis=0)[:N_HGRN]
    weights = (norm_ffn1, w_ffn1_up, w_ffn1_down, norm_mix, norm_ffn2, w_ffn2_up, w_ffn2_down,
               w_hgrn_in, lower_bounds, hgrn_norm, w_hgrn_out, w_conv_in, conv_w, w_conv_out, norm_final)
    b_prompt = x_prompt.shape[0]
    zero_hgrn = jnp.zeros((N_HGRN, b_prompt, HGRN_HEADS, HGRN_DK, HGRN_DV), x_prompt.dtype)
    zero_conv = jnp.zeros((N_CONV, b_prompt, CONV_W - 1, D_MODEL), x_prompt.dtype)
    y_prompt, state_hgrn_prompt, state_conv_prompt = decoder_trunk(x_prompt, zero_hgrn, zero_conv, *weights)
    y_sample, state_hgrn_sample, state_conv_sample = decoder_trunk(x_sample, state_hgrn, state_conv, *weights)
    return (y_prompt, y_sample, state_hgrn_prompt, state_hgrn_sample, state_conv_prompt, state_conv_sample)
```

```python
import contextlib
import numpy as np
import concourse.bass as bass
import concourse.mybir as mybir
from concourse.bass_utils import run_bass_kernel_spmd

F32 = mybir.dt.float32
BF16 = mybir.dt.bfloat16
AF = mybir.ActivationFunctionType
ALU = mybir.AluOpType

NCORES = 8
D = 1024
DFF = 2816
NT = 2176
SLICES = [(0, 512), (512, 512), (1024, 512), (1536, 512), (2048, 128)]
GROUPS = [[0, 1], [2, 3, 4]]
GSTART = [0, 1024]
GMAX = 1152
EPS = 1e-6
NS = 5
(V_NF1, V_NMIX, V_NF2, V_NFIN, V_HNORM, V_CW, V_LB) = (0, 2, 4, 6, 7, 8, 11)
NV = 13


def XO(c):
    return c if c < 1024 else c - 1024


def XK(s):
    return s % 2 if s < 4 else 2
HSTOP = 99
DBG_GROUPS = [0, 1]
SMODE = 0


class _Res:
    __slots__ = ("w", "r", "frozen")

    def __init__(self):
        self.w = None
        self.r = []
        self.frozen = False


class _Op:
    __slots__ = ("eng", "fn", "deps", "dsem", "sig", "cnt", "waits", "cost", "tset")

    def __init__(self, eng, fn, deps, dsem, cost=0.3, tset=None):
        self.cost = cost
        self.tset = tset
        self.eng = eng
        self.fn = fn
        self.deps = deps
        self.dsem = dsem
        self.sig = dsem is not None
        self.cnt = 0
        self.waits = []


class Sched:
    ENGS = ("pe", "act", "dve", "pool", "sp")

    def __init__(self, nc):
        self.nc = nc
        self.ops = []
        self.res = {}
        self.n_dsem = 0
        self.dsem_last = {}
        self.marks = []

    def new_dsem(self):
        self.n_dsem += 1
        return self.n_dsem - 1

    def _R(self, k):
        r = self.res.get(k)
        if r is None:
            r = self.res[k] = _Res()
        return r

    def freeze(self, k):
        self._R(k).frozen = True

    def op(self, eng, fn, reads=(), writes=(), dsem=None, cost=0.3, tset=None):
        deps = set()
        for k in reads:
            r = self._R(k)
            if r.w is not None:
                deps.add(r.w)
        for k in writes:
            r = self._R(k)
            if r.w is not None:
                deps.add(r.w)
            deps.update(r.r)
        if dsem is not None and dsem in self.dsem_last:
            deps.add(self.dsem_last[dsem])
        oid = len(self.ops)
        self.ops.append(_Op(eng, fn, deps, dsem, cost, tset))
        if dsem is not None:
            self.dsem_last[dsem] = oid
        for k in reads:
            r = self._R(k)
            if not r.frozen:
                r.r.append(oid)
        for k in writes:
            r = self._R(k)
            r.w = oid
            r.r = []
        return oid


    def reorder(self, window=96):
        ops = self.ops
        n = len(ops)
        by_eng = {e: [i for i, o in enumerate(ops) if o.eng == e] for e in self.ENGS}
        blv = [0.0] * n
        succ_max = [0.0] * n
        for i in range(n - 1, -1, -1):
            o = ops[i]
            c = o.cost + (1.5 if o.dsem is not None else 0.0)
            blv[i] = c + succ_max[i]
            for d in o.deps:
                if blv[i] > succ_max[d]:
                    succ_max[d] = blv[i]
        head = {e: 0 for e in self.ENGS}
        done = [False] * n
        fin = [0.0] * n
        t_eng = {e: 0.0 for e in self.ENGS}
        dma_free = 0.0
        cur_set = [None]
        order = []
        sched_cnt = 0
        while sched_cnt < n:
            best = None
            for e in self.ENGS:
                lst = by_eng[e]
                h = head[e]
                while h < len(lst) and done[lst[h]]:
                    h += 1
                head[e] = h
                cnt = 0
                k = h
                while k < len(lst) and cnt < window:
                    i = lst[k]
                    k += 1
                    if done[i]:
                        continue
                    cnt += 1
                    o = ops[i]
                    ok = True
                    st_t = t_eng[e]
                    for d in o.deps:
                        if not done[d]:
                            ok = False
                            break
                        fd = fin[d] + (0.0 if ops[d].eng == e and ops[d].dsem is None else 0.15)
                        if fd > st_t:
                            st_t = fd
                    if not ok:
                        continue
                    if o.tset is not None and o.tset != cur_set[0]:
                        st_t += 1.3
                    key = (round(st_t / PRIO_EPS), -blv[i], i)
                    if best is None or key < best[0]:
                        best = (key, i, e, st_t)
            assert best is not None, "scheduler deadlock"
            _, i, e, st_t = best
            o = ops[i]
            if o.dsem is not None:
                t_eng[e] = st_t + 0.06
                beg = max(st_t, dma_free)
                dma_free = beg + o.cost
                fin[i] = beg + o.cost + 1.5
            else:
                t_eng[e] = st_t + o.cost
                fin[i] = st_t + o.cost
                if o.tset is not None:
                    cur_set[0] = o.tset
            done[i] = True
            order.append(i)
            sched_cnt += 1
        prev = 0
        for name, idx in self.marks:
            if idx > prev:
                t_end = max(fin[prev:idx])
                busy = {e: sum(ops[i].cost for i in range(prev, idx) if ops[i].eng == e and ops[i].dsem is None)
                        for e in ("pe", "act", "dve")}
                print("[sched] %-14s ends %8.1f  work pe/act/dve %6.1f %6.1f %6.1f" % (
                    name, t_end, busy["pe"], busy["act"], busy["dve"]))
            prev = idx
        remap = {old: new for new, old in enumerate(order)}
        new_ops = [ops[i] for i in order]
        for o in new_ops:
            o.deps = set(remap[d] for d in o.deps)
        self.ops = new_ops
        return max(fin)

    def lower(self, stack):
        nc = self.nc
        ops = self.ops
        for o in ops:
            last = {}
            for d in o.deps:
                p = ops[d]
                if p.dsem is not None:
                    continue
                if p.eng == o.eng and p.eng == "pe":
                    continue
                if p.eng not in last or d > last[p.eng]:
                    last[p.eng] = d
            for d in last.values():
                ops[d].sig = True
        esem = {e: stack.enter_context(nc.semaphore("se_" + e)) for e in self.ENGS}
        dsems = [stack.enter_context(nc.semaphore("sd%d" % i)) for i in range(self.n_dsem)]
        ecnt = {e: 0 for e in self.ENGS}
        dcnt = [0] * self.n_dsem
        for o in ops:
            if o.dsem is not None:
                dcnt[o.dsem] += 16
                o.cnt = dcnt[o.dsem]
            elif o.sig:
                ecnt[o.eng] += 1
                o.cnt = ecnt[o.eng]
        waited = {e: {} for e in self.ENGS}
        for o in ops:
            need = {}
            for d in o.deps:
                p = ops[d]
                if p.dsem is not None:
                    key = ("d", p.dsem)
                else:
                    if p.eng == o.eng and p.eng == "pe":
                        continue
                    if not p.sig:
                        continue
                    key = ("e", p.eng)
                if need.get(key, 0) < p.cnt:
                    need[key] = p.cnt
            w = waited[o.eng]
            for key, v in need.items():
                if w.get(key, 0) < v:
                    w[key] = v
                    o.waits.append((key, v))
        streams = {e: [o for o in ops if o.eng == e] for e in self.ENGS}

        def sem_of(key):
            return dsems[key[1]] if key[0] == "d" else esem[key[1]]

        def run(engobj, ename):
            for o in streams[ename]:
                for key, v in o.waits:
                    engobj.wait_ge(sem_of(key), v)
                inst = o.fn(engobj)
                if o.dsem is not None:
                    inst.then_inc(dsems[o.dsem], 16)
                elif o.sig:
                    inst.then_inc(esem[ename], 1)
            if ename == "sp":
                for i in range(self.n_dsem):
                    if dcnt[i] > 0:
                        engobj.wait_ge(dsems[i], dcnt[i])

        with nc.Block() as block:
            @block.tensor
            def _(e):
                run(e, "pe")

            @block.scalar
            def _(e):
                run(e, "act")

            @block.vector
            def _(e):
                run(e, "dve")

            @block.gpsimd
            def _(e):
                run(e, "pool")

            @block.sync
            def _(e):
                run(e, "sp")
        return {e: len(streams[e]) for e in self.ENGS}


_MIXKEYS = {}
REORDER = 1
PRIO_EPS = 0.25
SSET = (2, 1)


def build_program(stage=6, collect=False):
    if not collect and stage not in _MIXKEYS:
        _MIXKEYS[stage] = build_program(stage, collect=True)
    nc = bass.Bass("TRN2", target_bir_lowering=False)

    def din(name, shape):
        return nc.dram_tensor(name, shape, F32, kind="ExternalInput").ap()

    def dout(name, shape):
        return nc.dram_tensor(name, shape, F32, kind="ExternalOutput").ap()

    x_d = din("x", [NT, D])
    sh_d = din("sh", [16, 8, 128, 128])
    sc_d = din("sc", [32, D])
    vecs_d = din("vecs", [128, NV * 8])
    w_ffn_up = [din("w_ffn1_up", [2, D, 2 * DFF]), din("w_ffn2_up", [2, D, 2 * DFF])]
    w_ffn_dn = [din("w_ffn1_down", [2, DFF, D]), din("w_ffn2_down", [2, DFF, D])]
    w_hin = din("w_hgrn_in", [1, D, 4 * D])
    w_hout = din("w_hgrn_out", [1, D, D])
    w_cin = din("w_conv_in", [1, D, 3 * D])
    w_cout = din("w_conv_out", [1, D, D])
    y_d = dout("y", [NT, D])
    shp_d = dout("shp", [8, 128, 128])
    shs_d = dout("shs", [16, 8, 128, 128])
    scp_d = dout("scp", [2, D])
    scs_d = dout("scs", [32, D])

    st = contextlib.ExitStack()
    with st:
        S = Sched(nc)

        def sb(name, shape, dt):
            return st.enter_context(nc.sbuf_tensor(name, shape, dt))

        xT = sb("xT", [128, 8, GMAX], F32)
        hT = sb("hT", [128, 8, GMAX], BF16)
        scr = sb("scr", [128, 22 * GMAX // 2], F32)
        aT = scr[:, :].bitcast(BF16).rearrange("p (f t) -> p f t", f=22)
        wslot = [sb("w%d" % i, [128, 8, 256], BF16) for i in range(NS)]
        ystage = [sb("ys%d" % i, [128, D], F32) for i in range(2)]
        ynT = sb("ynT", [128, 8, 128], F32)
        sqb = [sb("sqb%d" % i, [128, 512], BF16) for i in range(2)]
        lt = sb("lt", [128, 512], F32)
        rstd = sb("rstd", [128, 512], F32)
        sil = [sb("sil%d" % i, [128, 512], F32) for i in range(2)]
        identf = sb("identf", [128, 128], F32)
        identb = sb("identb", [128, 128], BF16)
        onesb = sb("onesb", [128, 128], BF16)
        maskU = sb("maskU", [128, 128], F32)
        maskS = sb("maskS", [128, 128], F32)
        rmP = sb("rmP", [128, 8, 64], F32)
        rmS = sb("rmS", [128, 16, 8], F32)
        RM = sb("RM", [128, 16], F32)
        mrow = sb("mrow", [128, 2], F32)
        vec = sb("vec", [128, NV, 8], F32)
        epst = sb("epst", [128, 1], F32)
        lbc = sb("lbc", [128, 6, 8], F32)
        dummy = sb("dummyt", [128, 4], F32)
        Sst = sb("Sst", [128, 8, 128], F32)
        Sbb = sb("Sbb", [128, 8, 128], BF16)
        ebl = sb("ebl", [128, 8, 16], F32)
        am = [sb("am%d" % i, [128, 128], BF16) for i in range(4)]
        kht_s = sb("kht_s", [128, 1024], BF16)
        S0bt = [sb("S0bt0", [128, 1024], BF16)] * 2
        vt_s = sb("vt_s", [128, 1024], BF16)
        vmask = [sb("vmask0", [128, 1024], BF16)] * 2
        sgN = sb("sgN", [128, 2, 512], BF16)
        Tb = [sb("Tb%d" % i, [128, 512], F32) for i in range(4)]
        Thn = sb("Thn", [128, 512], F32)
        kht_b = sb("kht_b", [128, 1024], BF16)
        khtH_b = sb("khtH_b", [128, 1024], BF16)
        SbAll_b = sb("SbAll_b", [128, 2, 8, 128], BF16)
        Spong_b = sb("Spong_b", [128, 2, 128], F32)
        bufA3 = sb("bufA3", [128, 1024], F32)
        bufB3 = sb("bufB3", [128, 1024], F32)
        vt3 = sb("vt3", [128, 1024], BF16)
        qtb_s = sb("qtb_s", [128, 8, 128], BF16)
        ebl_s = sb("ebl_s", [128, 8, 16], F32)
        sg_s = sb("sg_s", [128, 8, 128], BF16)
        carry = sb("carry", [128, 8, 2], F32)
        cso = sb("cso", [128, 8, 32], F32)

        _off = [0]

        def carve(nelem_f32, dt, pattern=None, **kw):
            a = scr[:, _off[0]:_off[0] + nelem_f32]
            _off[0] += nelem_f32
            if dt is BF16:
                a = a.bitcast(BF16)
            if pattern:
                a = a.rearrange(pattern, **kw)
            return a

        bufA2 = [carve(1024, F32) for _ in range(2)]
        bufB2 = [carve(1024, F32) for _ in range(2)]
        ktT = carve(512, BF16)
        khT = carve(512, BF16)
        qtT = carve(512, BF16)
        v_tok2 = [carve(512, BF16) for _ in range(2)]
        kh_tok = carve(512, BF16)
        sgT = carve(2048, BF16, "p (h t) -> p h t", h=8)
        T = [carve(512, F32) for _ in range(4)]
        q32raw = carve(1024, F32)
        qt32_s = q32raw.rearrange("p (h t) -> p h t", h=8)
        SbAll = q32raw.bitcast(BF16).rearrange("p (h c v) -> p h c v", h=2, c=8)
        Spong = carve(256, F32, "p (h v) -> p h v", h=2)
        hg_end = _off[0]
        _off[0] = 0
        ue = carve(8 * 514, F32, "p (k t) -> p k t", k=8)
        zT = carve(2048, BF16, "p (k t) -> p k t", k=8)
        cvt = [carve(512, F32) for _ in range(2)]
        t1t = [carve(512, F32) for _ in range(2)]
        assert max(hg_end, _off[0]) <= 22 * GMAX // 2

        pbank = [st.enter_context(nc.psum_tensor("pb%d" % i, [128, 512], F32)) for i in range(8)]
        _pbi = [0]
        nrot = [8]

        def PB():
            i = _pbi[0] % nrot[0]
            _pbi[0] += 1
            return pbank[i], ("ps", i)

        mixkeys = set() if collect else set(_MIXKEYS[stage])

        def mk(*k):
            mixkeys.add(k)
            return k

        akeys = set()

        def barrier():
            keys = list(akeys | mixkeys)
            S.op("dve", lambda e: e.memset(dummy[:, 0:1], 0.0), writes=keys + ["dummy"])

        def fsz(ap):
            n = 1
            for d in ap.shape[1:]:
                n *= d
            return n

        def mm(out, lhsT, rhs, start, stop, reads, writes, skip=False):
            c = max(fsz(rhs), 48) / 2400.0 + 0.004
            if skip:
                S.op("pe", lambda e: e.matmul(out, lhsT=lhsT, rhs=rhs, start=start, stop=stop,
                                              skip_group_check=True), reads, writes, cost=c)
            else:
                S.op("pe", lambda e: e.matmul(out, lhsT=lhsT, rhs=rhs, start=start, stop=stop), reads, writes, cost=c)

        def tr(out, in_, ident, reads, writes):
            S.op("pe", lambda e: e.transpose(out=out, in_=in_, identity=ident), reads, writes, cost=0.25)

        def act(out, in_, func, reads, writes, scale=None, bias=None):
            kw = {}
            if scale is not None:
                kw["scale"] = scale
            if bias is not None:
                kw["bias"] = bias
            tset = {AF.Silu: "silu", AF.Tanh: "silu", AF.Ln: "lnexp", AF.Exp: "lnexp"}.get(func)
            S.op("act", lambda e: e.activation(out=out, in_=in_, func=func, **kw), reads, writes,
                 cost=0.2 + fsz(out) * 0.0008, tset=tset)

        def tt(out, in0, in1, op, reads, writes, eng="dve"):
            S.op(eng, lambda e: e.tensor_tensor(out=out, in0=in0, in1=in1, op=op), reads, writes,
                 cost=0.1 + fsz(out) * 0.00115)

        def ts(out, in0, s1, s2, op0, op1, reads, writes, eng="dve"):
            if op1 is None:
                S.op(eng, lambda e: e.tensor_scalar(out=out, in0=in0, scalar1=s1, scalar2=None, op0=op0),
                     reads, writes, cost=0.1 + fsz(out) * 0.0009)
            else:
                S.op(eng, lambda e: e.tensor_scalar(out=out, in0=in0, scalar1=s1, scalar2=s2, op0=op0, op1=op1),
                     reads, writes, cost=0.1 + fsz(out) * 0.0009)

        def stt(out, in0, scalar, in1, op0, op1, reads, writes):
            S.op("dve", lambda e: e.scalar_tensor_tensor(out=out, in0=in0, scalar=scalar, in1=in1,
                                                         op0=op0, op1=op1), reads, writes,
                 cost=0.12 + fsz(out) * 0.0012)

        def cp(eng, out, in_, reads, writes):
            if eng == "act":
                S.op("act", lambda e: e.copy(out=out, in_=in_), reads, writes, cost=0.18 + fsz(out) * 0.0007)
            else:
                S.op(eng, lambda e: e.tensor_copy(out=out, in_=in_), reads, writes, cost=0.1 + fsz(out) * 0.0008)

        def dma(eng, out, in_, reads, writes, dsem):
            nb = 128 * fsz(out) * (4 if out.dtype == F32 else 4)
            S.op(eng, lambda e: e.dma_start(out=out, in_=in_), reads, writes, dsem=dsem, cost=nb / 360e3)

        wsem = [S.new_dsem() for _ in range(NS)]
        _wi = [0]
        ns_eff = [NS]

        def load_w(W2d, k0, nk, c0):
            slot = _wi[0] % ns_eff[0]
            _wi[0] += 1
            Wv = W2d.rearrange("(kc p) n -> p kc n", p=128)
            dma("pool", wslot[slot][:, 0:nk, :], Wv[:, k0:k0 + nk, c0:c0 + 256], [], [("w", slot)], wsem[slot])
            return slot

        C = "const"
        d_vec = S.new_dsem()
        dma("sp", vec[:, :, :], vecs_d.rearrange("p (v k) -> p v k", v=NV), [], [C], d_vec)

        def pool(fn, writes=(C,), reads=()):
            S.op("pool", fn, reads, list(writes))

        pool(lambda e: e.memset(identf[:, :], 0.0))
        pool(lambda e: e.affine_select(out=identf[:, :], in_=identf[:, :], pattern=[[-1, 128]],
                                       compare_op=ALU.not_equal, fill=1.0, base=0, channel_multiplier=1))
        pool(lambda e: e.tensor_copy(out=identb[:, :], in_=identf[:, :]))
        pool(lambda e: e.memset(onesb[:, :], 1.0))
        pool(lambda e: e.memset(epst[:, :], EPS))
        pool(lambda e: e.memset(maskU[:, :], 1.0))
        pool(lambda e: e.affine_select(out=maskU[:, :], in_=maskU[:, :], pattern=[[1, 128]],
                                       compare_op=ALU.is_ge, fill=0.0, base=0, channel_multiplier=-1))
        pool(lambda e: e.tensor_copy(out=maskS[:, :], in_=maskU[:, :]))
        pool(lambda e: e.memset(maskU[0:64, 64:128], 0.0))
        for j in range(1, 16):
            pool(lambda e, j=j: e.affine_select(out=maskS[:, 8 * j:8 * j + 8], in_=maskS[:, 8 * j:8 * j + 8],
                                                pattern=[[0, 8]], compare_op=ALU.is_ge, fill=0.0,
                                                base=-8 * j, channel_multiplier=1))
        pool(lambda e: e.memset(rmP[:, :, :], 1.0))
        pool(lambda e: e.memset(rmP[:, :, 0:1], 0.0))
        pool(lambda e: e.memset(rmS[:, :, :], 1.0))
        pool(lambda e: e.memset(rmS[:, :, 0:1], 0.0))
        pool(lambda e: e.memset(mrow[0:64, 0:1], 1.0))
        pool(lambda e: e.memset(mrow[64:128, 0:1], 0.0))
        pool(lambda e: e.memset(mrow[0:64, 1:2], 0.0))
        pool(lambda e: e.memset(mrow[64:128, 1:2], 1.0))
        pool(lambda e: e.memset(RM[:, :], 1.0))
        pool(lambda e: e.affine_select(out=RM[:, :], in_=RM[:, :], pattern=[[-8, 16]], compare_op=ALU.is_ge,
                                       fill=0.0, base=0, channel_multiplier=1))
        pool(lambda e: e.affine_select(out=RM[:, :], in_=RM[:, :], pattern=[[8, 16]], compare_op=ALU.is_ge,
                                       fill=0.0, base=7, channel_multiplier=-1))
        pool(lambda e: e.memset(Sst[:, :, :], 0.0), writes=[("S", h) for h in range(8)])
        pool(lambda e: e.memset(Sbb[:, :, :], 0.0), writes=[("Sb", h) for h in range(8)])
        pool(lambda e: e.memset(carry[:, :, :], 0.0), writes=["carry"])
        tt(lbc[:, 0, :], vec[:, V_LB, :], vec[:, V_LB + 1, :], ALU.subtract, [C], ["lbc"])
        act(lbc[:, 5, :], lbc[:, 0, :], AF.Tanh, ["lbc"], ["lbc"], scale=0.5)
        ts(lbc[:, 0, :], lbc[:, 5, :], 0.5, 0.5, ALU.mult, ALU.add, ["lbc"], ["lbc"])
        ts(lbc[:, 1, :], lbc[:, 0, :], -0.5, 0.5, ALU.mult, ALU.add, ["lbc"], ["lbc"])
        tt(lbc[:, 2, :], lbc[:, 0, :], lbc[:, 1, :], ALU.add, ["lbc"], ["lbc"])
        ts(lbc[:, 3, :], lbc[:, 1, :], -1.0, None, ALU.mult, None, ["lbc"], ["lbc"])
        ts(lbc[:, 4, :], lbc[:, 2, :], -1.0, 1.0, ALU.mult, ALU.add, ["lbc"], [C, "lbc"])
        S.freeze(C)

        def slice_of_tile(i):
            return i // 4 if i < 16 else 4

        ys_ld = [S.new_dsem() for _ in range(2)]
        ys_st = [S.new_dsem() for _ in range(2)]
        def load_x(tiles):
          for i in tiles:
              s = slice_of_tile(i)
              sl = i % 2
              dma("sp", ystage[sl][:, :], x_d[i * 128:(i + 1) * 128, :], [], [("ys", sl)], ys_ld[sl])
              for half in range(2):
                  pb, pk = PB()
                  for q in range(4):
                      kc = half * 4 + q
                      tr(pb[:, q * 128:(q + 1) * 128], ystage[sl][:, kc * 128:(kc + 1) * 128], identf[:, :],
                         [("ys", sl), C], [pk])
                  cp("act" if half == 0 else "dve", xT[:, half * 4:half * 4 + 4, XO(i * 128):XO(i * 128) + 128],
                     pb[:, :].rearrange("p (k t) -> p k t", k=4), [pk],
                     [("x", kc, XK(s)) for kc in range(half * 4, half * 4 + 4)])


        if 0 in DBG_GROUPS:
            load_x(range(0, 8))

        def rstd_of(src_aps, src_keys, L, inv_n):
            pb, pk = PB()
            n = len(src_aps)
            for i, ap in enumerate(src_aps):
                act(sqb[i % 2][:, :L], ap, AF.Square, [src_keys[i]], [("sq", i % 2)])
                mm(pb[:, :L], onesb[:, :], sqb[i % 2][:, :L], i == 0, i == n - 1, [("sq", i % 2), C], [pk])
            act(lt[:, :L], pb[:, :L], AF.Ln, [pk, C], ["lt"], scale=inv_n, bias=epst[:, 0:1])
            act(rstd[:, :L], lt[:, :L], AF.Exp, ["lt"], ["rstd"], scale=-0.5)

        def norm_to_h(g, s, gv):
            s0, L = SLICES[s]
            off = s0 - GSTART[g]
            rstd_of([xT[:, kc, XO(s0):XO(s0) + L] for kc in range(8)], [("x", kc, XK(s)) for kc in range(8)], L, 1.0 / D)
            for kc in range(8):
                stt(hT[:, kc, off:off + L], xT[:, kc, XO(s0):XO(s0) + L], vec[:, gv, kc:kc + 1], rstd[:, :L],
                    ALU.mult, ALU.mult, [("x", kc, XK(s)), "rstd", C], [("h", kc, s)])

        def ffn(layer, which, g):
            gv = (V_NF1 if which == 0 else V_NF2) + layer
            for s in GROUPS[g]:
                norm_to_h(g, s, gv)
            Wup = w_ffn_up[which][layer]
            Wdn = w_ffn_dn[which][layer]
            cnt = 0
            for u in range(11):
                sg_ = load_w(Wup, 0, 8, u * 256)
                su_ = load_w(Wup, 0, 8, DFF + u * 256)
                for fi in range(2):
                    f = 2 * u + fi
                    for s in GROUPS[g]:
                        s0, L = SLICES[s]
                        off = s0 - GSTART[g]
                        pg, pgk = PB()
                        for kc in range(8):
                            mm(pg[:, :L], wslot[sg_][:, kc, fi * 128:(fi + 1) * 128], hT[:, kc, off:off + L],
                               kc == 0, kc == 7, [("w", sg_), ("h", kc, s)], [pgk])
                        pu, puk = PB()
                        for kc in range(8):
                            mm(pu[:, :L], wslot[su_][:, kc, fi * 128:(fi + 1) * 128], hT[:, kc, off:off + L],
                               kc == 0, kc == 7, [("w", su_), ("h", kc, s)], [puk])
                        si = cnt % 2
                        cnt += 1
                        act(sil[si][:, :L], pg[:, :L], AF.Silu, [pgk], [("sil", si)])
                        akeys.add(("a", f, s))
                        tt(aT[:, f, off:off + L], sil[si][:, :L], pu[:, :L], ALU.mult, [("sil", si), puk],
                           [("a", f, s)])
            for v in range(4):
                sl3 = [load_w(Wdn, 0, 8, v * 256), load_w(Wdn, 8, 8, v * 256), load_w(Wdn, 16, 6, v * 256)]
                for oi in range(2):
                    kco = 2 * v + oi
                    for s in GROUPS[g]:
                        s0, L = SLICES[s]
                        off = s0 - GSTART[g]
                        pb, pk = PB()
                        for kk in range(22):
                            ws = sl3[kk // 8]
                            mm(pb[:, :L], wslot[ws][:, kk % 8, oi * 128:(oi + 1) * 128], aT[:, kk, off:off + L],
                               kk == 0, kk == 21, [("w", ws), ("a", kk, s)], [pk])
                        stt(xT[:, kco, XO(s0):XO(s0) + L], pb[:, :L], 0.5, xT[:, kco, XO(s0):XO(s0) + L], ALU.mult, ALU.add,
                            [pk, ("x", kco, XK(s))], [("x", kco, XK(s))])

        s0f_ld = [S.new_dsem() for _ in range(2)]
        s0f_st = [S.new_dsem() for _ in range(2)]
        d_shp = S.new_dsem()
        Win = w_hin[0]

        def merge_gen(ga, na, gr, nr):
            ia = ir = 0
            a_done = ga is None
            r_done = gr is None
            while not (a_done and r_done):
                pick_a = (not a_done) and (r_done or ia * nr <= ir * na)
                if pick_a:
                    try:
                        next(ga)
                        ia += 1
                        yield
                    except StopIteration:
                        a_done = True
                else:
                    try:
                        next(gr)
                        ir += 1
                        yield
                    except StopIteration:
                        r_done = True

        def drain(gen):
            for _ in gen:
                pass

        kt1 = wslot[3][:, 0:4, :].rearrange("p a b -> p (a b)")
        kh1 = wslot[3][:, 4:8, :].rearrange("p a b -> p (a b)")
        qt1 = wslot[4][:, 0:4, :].rearrange("p a b -> p (a b)")

        def hviews(L, bs):
            b3, b2 = bs
            HB = 1024 // L
            NP = L // 128
            kt_, kh_, qt_ = (ktT, khT, qtT) if b2 == 0 else (kt1, kh1, qt1)
            bA_ = bufA2[b3] if b3 < 2 else bufA3[:, :]
            bB_ = bufB2[b3] if b3 < 2 else bufB3[:, :]
            vt_ = v_tok2[b3] if b3 < 2 else vt3[:, :]
            kht_ = kh_tok if b2 == 0 else kht_b[:, :]
            khtH_ = khtH if b2 == 0 else khtH_b[:, :].rearrange("p (n c) -> p n c", n=4)
            v = dict(
                bA=bA_.rearrange("p (h t) -> p h t", h=HB),
                bB=bB_.rearrange("p (h t) -> p h t", h=HB),
                kt=kt_.rearrange("p (h t) -> p h t", h=HB),
                kh=kh_.rearrange("p (h t) -> p h t", h=HB),
                qt=qt_.rearrange("p (h t) -> p h t", h=HB),
                vt=vt_.rearrange("p (n c) -> p n c", n=NP),
                kht=kht_.rearrange("p (n c) -> p n c", n=NP),
                khtH=khtH_,
                sball=SbAll if b2 == 0 else SbAll_b,
                pong=Spong if b2 == 0 else Spong_b,
            )
            return v

        def hk(name, bs, hl, L):
            b3, b2 = bs
            q = hl if L == 512 else hl // 4
            if name in ("bA", "bB"):
                return mk(name, b3, q)
            if b2 == 1:
                return ("w", 3) if name in ("kt", "kh") else ("w", 4)
            return mk(name, q)

        def genA(g, s, bi, bs, kinds=("q", "g", "i", "f")):
            s0, L = SLICES[s]
            off = s0 - GSTART[g]
            sample = (s == 4)
            HB = 1024 // L
            NP = L // 128
            hb0 = bi * HB
            V = hviews(L, bs)
            if bi == 0 and sample:
                norm_to_h(g, s, V_NMIX + 0)
                yield
            vt = vt_s[:, :].rearrange("p (n c) -> p n c", n=1) if sample else V["vt"]
            vtk = mk("vts") if sample else mk("vt", bs[0])
            gi = 0
            for kind, base in (("q", 0), ("g", 3 * D), ("i", 2 * D), ("f", D)):
                if kind not in kinds:
                    continue
                for uu in range(HB // 2):
                    ws = load_w(Win, 0, 8, base + (hb0 + 2 * uu) * 128)
                    if kind == "i":
                        for p in range(NP):
                            pb, pk = PB()
                            for kc in range(8):
                                mm(pb[:, 0:256], hT[:, kc, off + p * 128:off + (p + 1) * 128],
                                   wslot[ws][:, kc, :], kc == 0, kc == 7, [("w", ws), ("h", kc, s)], [pk])
                            cp("act" if p % 2 == 0 else "dve", vt[:, p, uu * 256:(uu + 1) * 256],
                               pb[:, 0:256], [pk], [vtk])
                            yield
                        continue
                    for hi in range(2):
                        hl = 2 * uu + hi
                        h = hb0 + hl
                        pb, pk = PB()
                        for kc in range(8):
                            mm(pb[:, :L], wslot[ws][:, kc, hi * 128:(hi + 1) * 128], hT[:, kc, off:off + L],
                               kc == 0, kc == 7, [("w", ws), ("h", kc, s)], [pk])
                        if kind == "f":
                            act(V["bA"][:, hl, :], pb[:, :L], AF.Tanh, [pk], [hk("bA", bs, hl, L)], scale=0.5)
                        elif kind == "q":
                            act(V["bB"][:, hl, :], pb[:, :L], AF.Silu, [pk], [hk("bB", bs, hl, L)])
                        else:
                            si = gi % 2
                            gi += 1
                            act(sil[si][:, :L], pb[:, :L], AF.Tanh, [pk], [("sil", si)], scale=0.5)
                            if sample:
                                ts(sg_s[:, h, :], sil[si][:, :L], 0.5, 0.5, ALU.mult, ALU.add, [("sil", si)],
                                   [("sgs", h)])
                            elif bi == 0:
                                ts(sgN[:, hl, :L], sil[si][:, :L], 0.5, 0.5, ALU.mult, ALU.add, [("sil", si)],
                                   [("sgn", hl)])
                            else:
                                ts(sgT[:, h, :L], sil[si][:, :L], 0.5, 0.5, ALU.mult, ALU.add, [("sil", si)],
                                   [mk("sg", h)])
                        yield

        def stageB(s, bs, hb0, hl, V):
            s0, L = SLICES[s]
            sample = (s == 4)
            CS = 8 if sample else 64
            NC = L // CS
            rm = (rmS[:, :, :].rearrange("p a b -> p (a b)") if sample
                  else rmP[:, :, :].rearrange("p a b -> p (a b)"))
            h = hb0 + hl
            c1 = lbc[:, 1, h:h + 1]
            c0 = lbc[:, 2, h:h + 1]
            k1 = lbc[:, 3, h:h + 1]
            k0 = lbc[:, 4, h:h + 1]
            TT = T if hl % 2 == 0 else Tb
            T1, T2, T3, T4 = (TT[0][:, :L], TT[1][:, :L], TT[2][:, :L], TT[3][:, :L])
            K = [mk("T", hl % 2, i) for i in range(4)]
            kA = hk("bA", bs, hl, L)
            kB = hk("bB", bs, hl, L)
            act(T1, V["bA"][:, hl, :], AF.Ln, [kA, C], [K[0]], scale=c1, bias=c0)
            S.op("dve", lambda e: e.tensor_tensor_scan(out=T2, data0=rm[:, :L], data1=T1, initial=0.0,
                                                       op0=ALU.mult, op1=ALU.add), [K[0], C], [K[1]],
                 cost=0.1 + L * 0.0022)
            act(T3, T2, AF.Exp, [K[1]], [K[2]])
            act(T1, T2, AF.Exp, [K[1]], [K[0]], scale=-1.0)
            b3 = T2.rearrange("p (c t) -> p c t", t=CS)
            tt(T4.rearrange("p (c t) -> p c t", t=CS),
               b3[:, :, CS - 1:CS].broadcast_to([128, NC, CS]), b3, ALU.subtract, [K[1]], [K[3]])
            act(T4, T4, AF.Exp, [K[3]], [K[3]])
            eb_t, ebk = (ebl_s, ("ebls", h)) if sample else (ebl, ("ebl", h))
            cp("dve", eb_t[:, h, 0:NC].unsqueeze(2),
               T3.rearrange("p (c t) -> p c t", t=CS)[:, :, CS - 1:CS], [K[2]], [ebk])
            act(T2, V["bA"][:, hl, :], AF.Identity, [kA, C], [K[1]], scale=k1, bias=k0)
            tt(V["kt"][:, hl, :], T2, T1, ALU.mult, [K[1], K[0]], [hk("kt", bs, hl, L)])
            tt(V["kh"][:, hl, :], T2, T4, ALU.mult, [K[1], K[3]], [hk("kh", bs, hl, L)])
            tt(V["qt"][:, hl, :], V["bB"][:, hl, :], T3, ALU.mult, [kB, K[2]], [hk("qt", bs, hl, L)])
            if sample:
                tt(qtb_s[:, h, :], V["bB"][:, hl, :], T3, ALU.mult, [kB, K[2]], [mk("q32s")])

        def headnorm(L, o_ap, o_key, h, sg_ap, sg_key, sgi_ap=None, sgi_key=None):
            if sgi_ap is None:
                sgi_ap, sgi_key = sg_ap, sg_key
            rstd_of([o_ap], [o_key], L, 1.0 / 128)
            stt(Thn[:, :L], o_ap, vec[:, V_HNORM, h:h + 1], rstd[:, :L], ALU.mult, ALU.mult,
                [o_key, "rstd", C], ["Thn"])
            tt(sg_ap, Thn[:, :L], sgi_ap, ALU.mult, ["Thn", sgi_key], [sg_key])

        def outproj(s, sg_of, sgkey_of):
            s0, L = SLICES[s]
            for v in range(4):
                ws = load_w(w_hout[0], 0, 8, v * 256)
                for oi in range(2):
                    kco = 2 * v + oi
                    pb, pk = PB()
                    for h in range(8):
                        mm(pb[:, :L], wslot[ws][:, h, oi * 128:(oi + 1) * 128], sg_of(h), h == 0, h == 7,
                           [("w", ws), sgkey_of(h)], [pk])
                    tt(xT[:, kco, XO(s0):XO(s0) + L], pb[:, :L], xT[:, kco, XO(s0):XO(s0) + L], ALU.add, [pk, ("x", kco, XK(s))],
                       [("x", kco, XK(s))])
                    yield

        khtH = Sbb[:, :, :].rearrange("p a b -> p (a b)").rearrange("p (n c) -> p n c", n=4)

        def RBh(s, bi, bs, hl):
            stageB(s, bs, bi * 2, hl, hviews(512, bs))

        def RT(s, bi, bs):
            L = 512
            V = hviews(L, bs)
            for p in range(4):
                pb, pk = PB()
                for q in range(2):
                    mm(pb[:, q * 128:(q + 1) * 128], V["kh"][:, q, p * 128:(p + 1) * 128], identb[:, :],
                       True, True, [hk("kh", bs, q, L), C], [pk])
                cp("act", V["kht"][0:64, p, 0:256], pb[0:64, 0:256], [pk], [mk("kht", bs[1])])
                cp("dve", V["khtH"][64:128, p, 0:256], pb[64:128, 0:256], [pk], [mk("khtH", bs[1])])
            for hl in range(2):
                for c in range(8):
                    p = c // 2
                    bk = 4 + hl * 2 + c // 4
                    kk_ = V["kht"] if c % 2 == 0 else V["khtH"]
                    mm(pbank[bk][:, (c % 4) * 128:(c % 4 + 1) * 128], kk_[:, p, hl * 128:(hl + 1) * 128],
                       V["vt"][:, p, hl * 128:(hl + 1) * 128], True, True, [mk("kht", bs[1]), mk("khtH", bs[1]), mk("vt", bs[0])],
                       [("ps", bk)])

        def RChain(s, bi, bs):
            hb0 = bi * 2
            V = hviews(512, bs)
            b2 = bs[1]
            for c in range(8):
                for hl in range(2):
                    h = hb0 + hl
                    bk = 4 + hl * 2 + c // 4
                    P_ = pbank[bk][:, (c % 4) * 128:(c % 4 + 1) * 128]
                    e_ap = ebl[:, h, c:c + 1]
                    pong, pkey = V["pong"][:, hl, :], mk("pong", b2, hl)
                    if c % 2 == 0:
                        src, skey, dst, dkey = Sst[:, h, :], ("S", h), pong, pkey
                    else:
                        src, skey, dst, dkey = pong, pkey, Sst[:, h, :], ("S", h)
                    cp("act", V["sball"][:, hl, c, :], src, [skey], [mk("sball", b2, hl, c)])
                    stt(dst, src, e_ap, P_, ALU.mult, ALU.add, [skey, ("ebl", h), ("ps", bk)], [dkey])

        def RO(s, bi, bs):
            L = 512
            V = hviews(L, bs)
            ami = 0
            for p in range(4):
                slots = {}
                for hl in range(2):
                    pa, pak = PB()
                    mm(pa[:, 0:128], V["kt"][:, hl, p * 128:(p + 1) * 128], V["qt"][:, hl, p * 128:(p + 1) * 128],
                       True, True, [hk("kt", bs, hl, L), hk("qt", bs, hl, L)], [pak])
                    a_ = ami % 4
                    ami += 1
                    slots[hl] = a_
                    tt(am[a_][:, :], pa[:, 0:128], maskU[:, :], ALU.mult, [pak, C], [("am", a_)])
                for hl in range(2):
                    a_ = slots[hl]
                    po, pok = PB()
                    mm(po[:, 0:128], V["vt"][:, p, hl * 128:(hl + 1) * 128], am[a_][:, :], True, False,
                       [mk("vt", bs[0]), ("am", a_)], [pok])
                    for c in range(2):
                        r0 = c * 64
                        mm(po[:, r0:r0 + 64], V["sball"][:, hl, 2 * p + c, :],
                           V["qt"][:, hl, p * 128 + r0:p * 128 + r0 + 64], False, c == 1,
                           [mk("sball", bs[1], hl, 2 * p + c), hk("qt", bs, hl, L)], [pok])
                    cp("act", V["bA"][:, hl, p * 128:(p + 1) * 128], po[:, 0:128], [pok], [hk("bA", bs, hl, L)])

        def RH(g, s, bi, bs):
            L = 512
            V = hviews(L, bs)
            for hl in range(2):
                h = bi * 2 + hl
                if bi == 0:
                    headnorm(L, V["bA"][:, hl, :], hk("bA", bs, hl, L), h, sgT[:, h, :L], mk("sg", h),
                             sgN[:, hl, :L], ("sgn", hl))
                else:
                    headnorm(L, V["bA"][:, hl, :], hk("bA", bs, hl, L), h, sgT[:, h, :L], mk("sg", h))
            if bi == 3:
                drain(outproj(s, lambda h: sgT[:, h, :L], lambda h: mk("sg", h)))
                if s == 3:
                    dma("sp", shp_d.rearrange("h k v -> k h v"), Sst[:, :, :], [("S", h) for h in range(8)], [],
                        d_shp)

        def take(gen, n):
            if gen is None:
                return
            for _ in range(n):
                try:
                    next(gen)
                except StopIteration:
                    return

        def prompt_pipeline(g, slices):
            batches = [(s, bi) for s in slices for bi in range(4)]
            n = len(batches)
            S.op("dve", lambda e: e.memset(kh_tok[64:128, :], 0.0), [], [mk("kht", 0)])
            S.op("dve", lambda e: e.memset(khtH[0:64, :, :], 0.0), [], [mk("khtH", 0)])
            S.op("dve", lambda e: e.memset(kht_b[64:128, :], 0.0), [], [mk("kht", 1)])
            S.op("dve", lambda e: e.memset(khtH_b[0:64, :], 0.0), [], [mk("khtH", 1)])
            norm_to_h(g, slices[0], V_NMIX + 0)

            def A(i, kinds):
                if i < n:
                    drain(genA(g, batches[i][0], batches[i][1], (i % 3, i % 2), kinds))

            def RB_(i, hl):
                if i < n:
                    RBh(batches[i][0], batches[i][1], (i % 3, i % 2), hl)

            A(0, ("q", "g", "i", "f"))
            RB_(0, 0)
            RB_(0, 1)
            A(1, ("q", "g", "i", "f"))
            for i, (s, bi) in enumerate(batches):
                bs = (i % 3, i % 2)
                RT(s, bi, bs)
                yield
                RChain(s, bi, bs)
                yield
                RB_(i + 1, 0)
                if bi == 1 and s != slices[-1]:
                    norm_to_h(g, slices[slices.index(s) + 1], V_NMIX + 0)
                yield
                late_g = (bi == 3)
                A(i + 2, ("q",) if late_g else ("q", "g"))
                yield
                RO(s, bi, bs)
                yield
                RB_(i + 1, 1)
                yield
                A(i + 2, ("i",))
                yield
                RH(g, s, bi, bs)
                yield
                if late_g:
                    A(i + 2, ("g",))
                A(i + 2, ("f",))
                yield

        def sample_front(g):
            s = 4
            L = 128
            V = hviews(L, SSET)
            drain(genA(g, s, 0, SSET))
            for h in range(8):
                stageB(s, SSET, 0, h, V)
            for h0 in (0, 4):
                pb, pk = PB()
                for q in range(4):
                    mm(pb[:, q * 128:(q + 1) * 128], V["kh"][:, h0 + q, :], identb[:, :], True, True,
                       [hk("kh", SSET, h0 + q, L), C], [pk])
                cp("act", kht_s[:, h0 * 128:(h0 + 4) * 128], pb[:, 0:512], [pk], [mk("khts")])
            for h in range(8):
                pa, pak = PB()
                mm(pa[:, 0:128], V["kt"][:, h, :], V["qt"][:, h, :], True, True,
                   [hk("kt", SSET, h, L), hk("qt", SSET, h, L)], [pak])
                a_ = h % 4
                tt(am[a_][:, :], pa[:, 0:128], maskS[:, :], ALU.mult, [pak, C], [("am", a_)])
                po, pok = pbank[6 + h // 4], ("ps", 6 + h // 4)
                mm(po[:, (h % 4) * 128:(h % 4 + 1) * 128], vt_s[:, h * 128:(h + 1) * 128], am[a_][:, :],
                   True, True, [mk("vts"), ("am", a_)], [pok])
                cp("act", ynT[:, h, :], po[:, (h % 4) * 128:(h % 4 + 1) * 128], [pok], ["os"])

        o_s = ynT

        def sample_back():
            s = 4
            L = 128

            def load(j):
                sl = j % 2
                dma("sp", ystage[sl][:, :].rearrange("p (h v) -> p h v", h=8), sh_d[j].rearrange("h k v -> k h v"),
                    [], [("ys", sl)], s0f_ld[sl])

            load(0)
            load(1)
            yield
            for j in range(16):
                sl = j % 2
                ts(vmask[sl][:, :], vt_s[:, :], RM[:, j:j + 1], None, ALU.mult, None, [mk("vts"), C], [mk("vm", 0)])
                cp("act", S0bt[sl][:, :], ystage[sl][:, :], [("ys", sl)], [("S0b", 0)])
                pi, pik = PB()
                for h in range(8):
                    mm(pi[:, h * 8:(h + 1) * 8], S0bt[sl][:, h * 128:(h + 1) * 128], qtb_s[:, h, 8 * j:8 * j + 8],
                       True, True, [("S0b", 0), mk("q32s")], [pik])
                tt(o_s[:, :, 8 * j:8 * j + 8], o_s[:, :, 8 * j:8 * j + 8],
                   pi[:, 0:64].rearrange("p (h t) -> p h t", t=8), ALU.add, ["os", pik], ["os"])
                for h in range(8):
                    S0 = ystage[sl][:, h * 128:(h + 1) * 128]
                    ps_, psk = PB()
                    mm(ps_[:, 0:128], kht_s[:, h * 128:(h + 1) * 128], vmask[sl][:, h * 128:(h + 1) * 128],
                       True, True, [mk("khts"), mk("vm", 0)], [psk])
                    stt(S0, S0, ebl_s[:, h, j:j + 1], ps_[:, 0:128], ALU.mult, ALU.add,
                        [("ys", sl), ("ebls", h), psk], [("ys", sl)])
                dma("sp", shs_d[j].rearrange("h k v -> k h v"),
                    ystage[sl][:, :].rearrange("p (h v) -> p h v", h=8), [("ys", sl)], [], s0f_st[sl])
                if j + 2 < 16:
                    load(j + 2)
                yield
            for h in range(8):
                headnorm(L, o_s[:, h, :], "os", h, sg_s[:, h, :], ("sgs", h))
            yield from outproj(s, lambda h: sg_s[:, h, :], lambda h: ("sgs", h))

        def hgrn(g):
            ns_eff[0] = 3
            nrot[0] = 4
            hgrn_(g)
            nrot[0] = 8
            ns_eff[0] = NS

        def hgrn_(g):
            slices = [s for s in GROUPS[g] if s != 4]
            pp = prompt_pipeline(g, slices)
            if 4 in GROUPS[g]:
                sample_front(g)
                npp = len(slices) * 4 * (11 + 30)
                if SMODE == 2:
                    drain(sample_back())
                    drain(pp)
                else:
                    drain(merge_gen(pp, 72, sample_back(), 26))
            else:
                drain(pp)

        d_sc = S.new_dsem()
        d_scs = S.new_dsem()
        d_scp = S.new_dsem()

        def conv(g):
            Win = w_cin[0]
            for s in GROUPS[g]:
                s0, L = SLICES[s]
                off = s0 - GSTART[g]
                sample = (s == 4)
                norm_to_h(g, s, V_NMIX + 1)

                def uview(kc, a, b):
                    if sample:
                        return ue[:, kc, 0:160].rearrange("p (j t) -> p j t", t=10)[:, :, a:a + 8]
                    return ue[:, kc, a:a + L]

                def v3(ap):
                    return ap.rearrange("p (j t) -> p j t", t=8) if sample else ap

                if sample:
                    dma("sp", ystage[0][0:32, :], sc_d[:, :], [], [("ys", 0)], d_sc)
                    pb, pk = PB()
                    for kc in range(8):
                        tr(pb[:, kc * 32:(kc + 1) * 32], ystage[0][0:32, kc * 128:(kc + 1) * 128], identf[0:32, 0:32],
                           [("ys", 0), C], [pk])
                    for kc in range(8):
                        cp("dve", ue[:, kc, 0:160].rearrange("p (j t) -> p j t", t=10)[:, :, 0:2],
                           pb[:, kc * 32:(kc + 1) * 32].rearrange("p (j t) -> p j t", t=2), [pk], [mk("ue", kc)])
                else:
                    for kc in range(8):
                        cp("dve", ue[:, kc, 0:2], carry[:, kc, :], ["carry"], [mk("ue", kc)])
                ci = 0
                for m in range(4):
                    wc = load_w(Win, 0, 8, D + m * 256)
                    wv = load_w(Win, 0, 8, 2 * D + m * 256)
                    wb = load_w(Win, 0, 8, m * 256)
                    for fi in range(2):
                        kc = 2 * m + fi
                        pc, pck = PB()
                        for k2 in range(8):
                            mm(pc[:, :L], wslot[wc][:, k2, fi * 128:(fi + 1) * 128], hT[:, k2, off:off + L],
                               k2 == 0, k2 == 7, [("w", wc), ("h", k2, s)], [pck])
                        pv, pvk = PB()
                        for k2 in range(8):
                            mm(pv[:, :L], wslot[wv][:, k2, fi * 128:(fi + 1) * 128], hT[:, k2, off:off + L],
                               k2 == 0, k2 == 7, [("w", wv), ("h", k2, s)], [pvk])
                        pb, pbk = PB()
                        for k2 in range(8):
                            mm(pb[:, :L], wslot[wb][:, k2, fi * 128:(fi + 1) * 128], hT[:, k2, off:off + L],
                               k2 == 0, k2 == 7, [("w", wb), ("h", k2, s)], [pbk])
                        i2 = ci % 2
                        ci += 1
                        cp("act", cvt[i2][:, :L], pc[:, :L], [pck], [mk("cv", i2)])
                        tt(uview(kc, 2, 0), v3(cvt[i2][:, :L]), v3(pv[:, :L]), ALU.mult, [mk("cv", i2), pvk],
                           [mk("ue", kc)])
                        t1 = v3(t1t[i2][:, :L])
                        ts(t1, uview(kc, 0, 0), vec[:, V_CW + 0, kc:kc + 1], None, ALU.mult, None,
                           [mk("ue", kc), C], [mk("t1", i2)])
                        stt(t1, uview(kc, 1, 0), vec[:, V_CW + 1, kc:kc + 1], t1, ALU.mult, ALU.add,
                            [mk("ue", kc), mk("t1", i2), C], [mk("t1", i2)])
                        stt(t1, uview(kc, 2, 0), vec[:, V_CW + 2, kc:kc + 1], t1, ALU.mult, ALU.add,
                            [mk("ue", kc), mk("t1", i2), C], [mk("t1", i2)])
                        tt(zT[:, kc, :L], t1t[i2][:, :L], pb[:, :L], ALU.mult, [mk("t1", i2), pbk], [mk("z", kc)])
                if sample:
                    for kc in range(8):
                        cp("dve", cso[:, kc, :].rearrange("p (j t) -> p j t", t=2),
                           ue[:, kc, 0:160].rearrange("p (j t) -> p j t", t=10)[:, :, 8:10], [mk("ue", kc)], ["cso"])
                    for half in range(2):
                        pb, pk = PB()
                        for q in range(4):
                            kc = half * 4 + q
                            tr(pb[0:32, q * 128:(q + 1) * 128], cso[:, kc, :], identf[:, :], ["cso", C], [pk])
                        cp("dve", ystage[1][0:32, half * 512:(half + 1) * 512], pb[0:32, :], [pk], [("ys", 1)])
                    dma("sp", scs_d[:, :], ystage[1][0:32, :], [("ys", 1)], [], d_scs)
                else:
                    for kc in range(8):
                        cp("dve", carry[:, kc, :], ue[:, kc, L:L + 2], [mk("ue", kc)], ["carry"])
                    if s == 3:
                        for half in range(2):
                            pb, pk = PB()
                            for q in range(4):
                                kc = half * 4 + q
                                tr(pb[0:2, q * 128:(q + 1) * 128], carry[:, kc, :], identf[:, :], ["carry", C], [pk])
                            cp("dve", ystage[1][0:2, half * 512:(half + 1) * 512], pb[0:2, :], [pk], [("ys", 1)])
                        dma("sp", scp_d[:, :], ystage[1][0:2, :], [("ys", 1)], [], d_scp)
                for v in range(4):
                    ws = load_w(w_cout[0], 0, 8, v * 256)
                    for oi in range(2):
                        kco = 2 * v + oi
                        pb, pk = PB()
                        for k2 in range(8):
                            mm(pb[:, :L], wslot[ws][:, k2, oi * 128:(oi + 1) * 128], zT[:, k2, :L], k2 == 0, k2 == 7,
                               [("w", ws), mk("z", k2)], [pk])
                        tt(xT[:, kco, XO(s0):XO(s0) + L], pb[:, :L], xT[:, kco, XO(s0):XO(s0) + L], ALU.add, [pk, ("x", kco, XK(s))],
                           [("x", kco, XK(s))])

        def final(g):
            for s in GROUPS[g]:
                s0, L = SLICES[s]
                rstd_of([xT[:, kc, XO(s0):XO(s0) + L] for kc in range(8)], [("x", kc, XK(s)) for kc in range(8)], L, 1.0 / D)
                for p in range(L // 128):
                    t0 = s0 + p * 128
                    for kc in range(8):
                        stt(ynT[:, kc, :], xT[:, kc, XO(t0):XO(t0) + 128], vec[:, V_NFIN, kc:kc + 1],
                            rstd[:, p * 128:(p + 1) * 128], ALU.mult, ALU.mult, [("x", kc, XK(s)), "rstd", C],
                            [("yn", kc), "os"])
                    sl = (t0 // 128) % 2
                    for half in range(2):
                        pb, pk = PB()
                        for q in range(4):
                            kc = half * 4 + q
                            tr(pb[:, q * 128:(q + 1) * 128], ynT[:, kc, :], identf[:, :], [("yn", kc), "os", C], [pk])
                        cp("act" if half == 0 else "dve", ystage[sl][:, half * 512:(half + 1) * 512], pb[:, :],
                           [pk], [("ys", sl)])
                    dma("sp", y_d[t0:t0 + 128, :], ystage[sl][:, :], [("ys", sl)], [], ys_st[sl])

        def mark(name):
            S.marks.append((name, len(S.ops)))

        mark("load")
        for g in DBG_GROUPS:
            if g == 1:
                load_x(range(8, 17))
            if stage >= 1:
                ffn(0, 0, g)
                mark("ffn1.L0.g%d" % g)
            if stage >= 2:
                barrier()
                hgrn(g)
                barrier()
                mark("hgrn.g%d" % g)
            if stage >= 3:
                ffn(0, 1, g)
                mark("ffn2.L0.g%d" % g)
            if stage >= 4:
                ffn(1, 0, g)
                mark("ffn1.L1.g%d" % g)
            if stage >= 5:
                barrier()
                conv(g)
                barrier()
                mark("conv.g%d" % g)
            if stage >= 6:
                ffn(1, 1, g)
                mark("ffn2.L1.g%d" % g)
            final(g)
            mark("final.g%d" % g)
        if collect:
            return set(mixkeys)
        if REORDER:
            est = S.reorder()
            print("[sched] estimated span us:", round(est, 1))
        counts = S.lower(st)
    return nc, counts


_CACHE = {}


def kernel(x_prompt, x_sample, state_hgrn, state_conv, norm_ffn1, w_ffn1_up, w_ffn1_down,
           norm_mix, norm_ffn2, w_ffn2_up, w_ffn2_down, w_hgrn_in, hgrn_lower_bounds,
           hgrn_norm, w_hgrn_out, w_conv_in, conv_w, w_conv_out, norm_final):
    f = lambda a: np.ascontiguousarray(np.asarray(a, dtype=np.float32))
    x_prompt, x_sample, state_hgrn, state_conv = f(x_prompt), f(x_sample), f(state_hgrn), f(state_conv)
    if "nc" not in _CACHE:
        _CACHE["nc"] = build_program()[0]
    nc = _CACHE["nc"]
    vlist = [f(norm_ffn1)[0], f(norm_ffn1)[1], f(norm_mix)[0], f(norm_mix)[1], f(norm_ffn2)[0], f(norm_ffn2)[1],
             f(norm_final), f(hgrn_norm)[0], f(conv_w)[0, 0], f(conv_w)[0, 1], f(conv_w)[0, 2],
             f(hgrn_lower_bounds)[0], f(hgrn_lower_bounds)[1]]
    vecs = np.stack(vlist).reshape(NV, 8, 128).transpose(2, 0, 1).reshape(128, NV * 8)
    vecs = np.ascontiguousarray(vecs)
    shared = {
        "vecs": vecs,
        "w_ffn1_up": f(w_ffn1_up), "w_ffn2_up": f(w_ffn2_up),
        "w_ffn1_down": f(w_ffn1_down), "w_ffn2_down": f(w_ffn2_down),
        "w_hgrn_in": f(w_hgrn_in), "w_hgrn_out": f(w_hgrn_out),
        "w_conv_in": f(w_conv_in), "w_conv_out": f(w_conv_out),
    }
    in_maps = []
    for c in range(NCORES):
        m = dict(shared)
        m["x"] = np.ascontiguousarray(np.concatenate(
            [x_prompt[c], x_sample[16 * c:16 * c + 16].reshape(128, D)], axis=0))
        m["sh"] = np.ascontiguousarray(state_hgrn[0, 16 * c:16 * c + 16])
        m["sc"] = np.ascontiguousarray(state_conv[0, 16 * c:16 * c + 16].reshape(32, D))
        in_maps.append(m)
    res = run_bass_kernel_spmd(nc, in_maps, core_ids=list(range(NCORES)))
    R = res.results
    y_prompt = np.stack([R[c]["y"][0:2048] for c in range(NCORES)])
    y_sample = np.concatenate([R[c]["y"][2048:].reshape(16, 8, D) for c in range(NCORES)], axis=0)
    shp = np.stack([R[c]["shp"] for c in range(NCORES)])[None]
    shs = np.concatenate([R[c]["shs"] for c in range(NCORES)], axis=0)[None]
    scp = np.stack([R[c]["scp"] for c in range(NCORES)])[None]
    scs = np.concatenate([R[c]["scs"].reshape(16, 2, D) for c in range(NCORES)], axis=0)[None]
    return (y_prompt.astype(np.float32), y_sample.astype(np.float32), shp.astype(np.float32),
            shs.astype(np.float32), scp.astype(np.float32), scs.astype(np.float32))
```

```python
import contextlib
import numpy as np
import concourse.bass as bass
import concourse.mybir as mybir
from concourse.bass_utils import run_bass_kernel_spmd

F32 = mybir.dt.float32
BF16 = mybir.dt.bfloat16
AF = mybir.ActivationFunctionType
ALU = mybir.AluOpType

NCORES = 8
D = 1024
DFF = 2816
NT = 2176
SLICES = [(0, 512), (512, 512), (1024, 512), (1536, 512), (2048, 128)]
GROUPS = [[0, 1], [2, 3, 4]]
GSTART = [0, 1024]
GMAX = 1152
EPS = 1e-6
NS = 5
(V_NF1, V_NMIX, V_NF2, V_NFIN, V_HNORM, V_CW, V_LB) = (0, 2, 4, 6, 7, 8, 11)
NV = 13


def XO(c):
    return c if c < 1024 else c - 1024


def XK(s):
    return s % 2 if s < 4 else 2
HSTOP = 99
DBG_GROUPS = [0, 1]
SMODE = 0


class _Res:
    __slots__ = ("w", "r", "frozen")

    def __init__(self):
        self.w = None
        self.r = []
        self.frozen = False


class _Op:
    __slots__ = ("eng", "fn", "deps", "dsem", "sig", "cnt", "waits", "cost", "tset")

    def __init__(self, eng, fn, deps, dsem, cost=0.3, tset=None):
        self.cost = cost
        self.tset = tset
        self.eng = eng
        self.fn = fn
        self.deps = deps
        self.dsem = dsem
        self.sig = dsem is not None
        self.cnt = 0
        self.waits = []


class Sched:
    ENGS = ("pe", "act", "dve", "pool", "sp")

    def __init__(self, nc):
        self.nc = nc
        self.ops = []
        self.res = {}
        self.n_dsem = 0
        self.dsem_last = {}
        self.marks = []

    def new_dsem(self):
        self.n_dsem += 1
        return self.n_dsem - 1

    def _R(self, k):
        r = self.res.get(k)
        if r is None:
            r = self.res[k] = _Res()
        return r

    def freeze(self, k):
        self._R(k).frozen = True

    def op(self, eng, fn, reads=(), writes=(), dsem=None, cost=0.3, tset=None):
        deps = set()
        for k in reads:
            r = self._R(k)
            if r.w is not None:
                deps.add(r.w)
        for k in writes:
            r = self._R(k)
            if r.w is not None:
                deps.add(r.w)
            deps.update(r.r)
        if dsem is not None and dsem in self.dsem_last:
            deps.add(self.dsem_last[dsem])
        oid = len(self.ops)
        self.ops.append(_Op(eng, fn, deps, dsem, cost, tset))
        if dsem is not None:
            self.dsem_last[dsem] = oid
        for k in reads:
            r = self._R(k)
            if not r.frozen:
                r.r.append(oid)
        for k in writes:
            r = self._R(k)
            r.w = oid
            r.r = []
        return oid


    def reorder(self, window=96):
        ops = self.ops
        n = len(ops)
        by_eng = {e: [i for i, o in enumerate(ops) if o.eng == e] for e in self.ENGS}
        blv = [0.0] * n
        succ_max = [0.0] * n
        for i in range(n - 1, -1, -1):
            o = ops[i]
            c = o.cost + (1.5 if o.dsem is not None else 0.0)
            blv[i] = c + succ_max[i]
            for d in o.deps:
                if blv[i] > succ_max[d]:
                    succ_max[d] = blv[i]
        head = {e: 0 for e in self.ENGS}
        done = [False] * n
        fin = [0.0] * n
        t_eng = {e: 0.0 for e in self.ENGS}
        dma_free = 0.0
        cur_set = [None]
        order = []
        sched_cnt = 0
        while sched_cnt < n:
            best = None
            for e in self.ENGS:
                lst = by_eng[e]
                h = head[e]
                while h < len(lst) and done[lst[h]]:
                    h += 1
                head[e] = h
                cnt = 0
                k = h
                while k < len(lst) and cnt < window:
                    i = lst[k]
                    k += 1
                    if done[i]:
                        continue
                    cnt += 1
                    o = ops[i]
                    ok = True
                    st_t = t_eng[e]
                    for d in o.deps:
                        if not done[d]:
                            ok = False
                            break
                        fd = fin[d] + (0.0 if ops[d].eng == e and ops[d].dsem is None else 0.15)
                        if fd > st_t:
                            st_t = fd
                    if not ok:
                        continue
                    if o.tset is not None and o.tset != cur_set[0]:
                        st_t += 1.3
                    key = (round(st_t / PRIO_EPS), -blv[i], i)
                    if best is None or key < best[0]:
                        best = (key, i, e, st_t)
            assert best is not None, "scheduler deadlock"
            _, i, e, st_t = best
            o = ops[i]
            if o.dsem is not None:
                t_eng[e] = st_t + 0.06
                beg = max(st_t, dma_free)
                dma_free = beg + o.cost
                fin[i] = beg + o.cost + 1.5
            else:
                t_eng[e] = st_t + o.cost
                fin[i] = st_t + o.cost
                if o.tset is not None:
                    cur_set[0] = o.tset
            done[i] = True
            order.append(i)
            sched_cnt += 1
        prev = 0
        for name, idx in self.marks:
            if idx > prev:
                t_end = max(fin[prev:idx])
                busy = {e: sum(ops[i].cost for i in range(prev, idx) if ops[i].eng == e and ops[i].dsem is None)
                        for e in ("pe", "act", "dve")}
                print("[sched] %-14s ends %8.1f  work pe/act/dve %6.1f %6.1f %6.1f" % (
                    name, t_end, busy["pe"], busy["act"], busy["dve"]))
            prev = idx
        remap = {old: new for new, old in enumerate(order)}
        new_ops = [ops[i] for i in order]
        for o in new_ops:
            o.deps = set(remap[d] for d in o.deps)
        self.ops = new_ops
        return max(fin)

    def lower(self, stack):
        nc = self.nc
        ops = self.ops
        for o in ops:
            for d in o.deps:
                p = ops[d]
                if p.dsem is not None:
                    continue
                if p.eng == o.eng and p.eng == "pe":
                    continue
                p.sig = True
        esem = {e: stack.enter_context(nc.semaphore("se_" + e)) for e in self.ENGS}
        dsems = [stack.enter_context(nc.semaphore("sd%d" % i)) for i in range(self.n_dsem)]
        ecnt = {e: 0 for e in self.ENGS}
        dcnt = [0] * self.n_dsem
        for o in ops:
            if o.dsem is not None:
                dcnt[o.dsem] += 16
                o.cnt = dcnt[o.dsem]
            elif o.sig:
                ecnt[o.eng] += 1
                o.cnt = ecnt[o.eng]
        waited = {e: {} for e in self.ENGS}
        for o in ops:
            need = {}
            for d in o.deps:
                p = ops[d]
                if p.dsem is not None:
                    key = ("d", p.dsem)
                else:
                    if p.eng == o.eng and p.eng == "pe":
                        continue
                    key = ("e", p.eng)
                if need.get(key, 0) < p.cnt:
                    need[key] = p.cnt
            w = waited[o.eng]
            for key, v in need.items():
                if w.get(key, 0) < v:
                    w[key] = v
                    o.waits.append((key, v))
        streams = {e: [o for o in ops if o.eng == e] for e in self.ENGS}

        def sem_of(key):
            return dsems[key[1]] if key[0] == "d" else esem[key[1]]

        def run(engobj, ename):
            for o in streams[ename]:
                for key, v in o.waits:
                    engobj.wait_ge(sem_of(key), v)
                inst = o.fn(engobj)
                if o.dsem is not None:
                    inst.then_inc(dsems[o.dsem], 16)
                elif o.sig:
                    inst.then_inc(esem[ename], 1)
            if ename == "sp":
                for i in range(self.n_dsem):
                    if dcnt[i] > 0:
                        engobj.wait_ge(dsems[i], dcnt[i])

        with nc.Block() as block:
            @block.tensor
            def _(e):
                run(e, "pe")

            @block.scalar
            def _(e):
                run(e, "act")

            @block.vector
            def _(e):
                run(e, "dve")

            @block.gpsimd
            def _(e):
                run(e, "pool")

            @block.sync
            def _(e):
                run(e, "sp")
        return {e: len(streams[e]) for e in self.ENGS}


_MIXKEYS = {}
REORDER = 1
PRIO_EPS = 0.25
SSET = (2, 1)


def build_program(stage=6, collect=False):
    if not collect and stage not in _MIXKEYS:
        _MIXKEYS[stage] = build_program(stage, collect=True)
    nc = bass.Bass("TRN2", target_bir_lowering=False)

    def din(name, shape):
        return nc.dram_tensor(name, shape, F32, kind="ExternalInput").ap()

    def dout(name, shape):
        return nc.dram_tensor(name, shape, F32, kind="ExternalOutput").ap()

    x_d = din("x", [NT, D])
    sh_d = din("sh", [16, 8, 128, 128])
    sc_d = din("sc", [32, D])
    vecs_d = din("vecs", [128, NV * 8])
    w_ffn_up = [din("w_ffn1_up", [2, D, 2 * DFF]), din("w_ffn2_up", [2, D, 2 * DFF])]
    w_ffn_dn = [din("w_ffn1_down", [2, DFF, D]), din("w_ffn2_down", [2, DFF, D])]
    w_hin = din("w_hgrn_in", [1, D, 4 * D])
    w_hout = din("w_hgrn_out", [1, D, D])
    w_cin = din("w_conv_in", [1, D, 3 * D])
    w_cout = din("w_conv_out", [1, D, D])
    y_d = dout("y", [NT, D])
    shp_d = dout("shp", [8, 128, 128])
    shs_d = dout("shs", [16, 8, 128, 128])
    scp_d = dout("scp", [2, D])
    scs_d = dout("scs", [32, D])

    st = contextlib.ExitStack()
    with st:
        S = Sched(nc)

        def sb(name, shape, dt):
            return st.enter_context(nc.sbuf_tensor(name, shape, dt))

        xT = sb("xT", [128, 8, GMAX], F32)
        hT = sb("hT", [128, 8, GMAX], BF16)
        scr = sb("scr", [128, 22 * GMAX // 2], F32)
        aT = scr[:, :].bitcast(BF16).rearrange("p (f t) -> p f t", f=22)
        wslot = [sb("w%d" % i, [128, 8, 256], BF16) for i in range(NS)]
        ystage = [sb("ys%d" % i, [128, D], F32) for i in range(2)]
        ynT = sb("ynT", [128, 8, 128], F32)
        sqb = [sb("sqb%d" % i, [128, 512], BF16) for i in range(2)]
        lt = sb("lt", [128, 512], F32)
        rstd = sb("rstd", [128, 512], F32)
        sil = [sb("sil%d" % i, [128, 512], F32) for i in range(3)]
        identf = sb("identf", [128, 128], F32)
        identb = sb("identb", [128, 128], BF16)
        onesb = sb("onesb", [128, 128], BF16)
        maskU = sb("maskU", [128, 128], F32)
        maskS = sb("maskS", [128, 128], F32)
        rmP = sb("rmP", [128, 8, 64], F32)
        rmS = sb("rmS", [128, 16, 8], F32)
        RM = sb("RM", [128, 16], F32)
        mrow = sb("mrow", [128, 2], F32)
        vec = sb("vec", [128, NV, 8], F32)
        epst = sb("epst", [128, 1], F32)
        lbc = sb("lbc", [128, 6, 8], F32)
        dummy = sb("dummyt", [128, 4], F32)
        Sst = sb("Sst", [128, 8, 128], F32)
        Sbb = sb("Sbb", [128, 8, 128], BF16)
        ebl = sb("ebl", [128, 8, 16], F32)
        am = [sb("am%d" % i, [128, 128], BF16) for i in range(4)]
        kht_s = sb("kht_s", [128, 1024], BF16)
        S0bt = [sb("S0bt0", [128, 1024], BF16)] * 2
        vt_s = sb("vt_s", [128, 1024], BF16)
        vmask = [sb("vmask0", [128, 1024], BF16)] * 2
        sgN = sb("sgN", [128, 2, 512], BF16)
        Tb = [sb("Tb%d" % i, [128, 512], F32) for i in range(4)]
        Thn = sb("Thn", [128, 512], F32)
        kht_b = sb("kht_b", [128, 1024], BF16)
        khtH_b = sb("khtH_b", [128, 1024], BF16)
        SbAll_b = sb("SbAll_b", [128, 2, 8, 128], BF16)
        Spong_b = sb("Spong_b", [128, 2, 128], F32)
        bufA3 = sb("bufA3", [128, 1024], F32)
        bufB3 = sb("bufB3", [128, 1024], F32)
        vt3 = sb("vt3", [128, 1024], BF16)
        qtb_s = sb("qtb_s", [128, 8, 128], BF16)
        ebl_s = sb("ebl_s", [128, 8, 16], F32)
        sg_s = sb("sg_s", [128, 8, 128], BF16)
        carry = sb("carry", [128, 8, 2], F32)
        cso = sb("cso", [128, 8, 32], F32)

        _off = [0]

        def carve(nelem_f32, dt, pattern=None, **kw):
            a = scr[:, _off[0]:_off[0] + nelem_f32]
            _off[0] += nelem_f32
            if dt is BF16:
                a = a.bitcast(BF16)
            if pattern:
                a = a.rearrange(pattern, **kw)
            return a

        bufA2 = [carve(1024, F32) for _ in range(2)]
        bufB2 = [carve(1024, F32) for _ in range(2)]
        ktT = carve(512, BF16)
        khT = carve(512, BF16)
        qtT = carve(512, BF16)
        v_tok2 = [carve(512, BF16) for _ in range(2)]
        kh_tok = carve(512, BF16)
        sgT = carve(2048, BF16, "p (h t) -> p h t", h=8)
        T = [carve(512, F32) for _ in range(4)]
        q32raw = carve(1024, F32)
        qt32_s = q32raw.rearrange("p (h t) -> p h t", h=8)
        SbAll = q32raw.bitcast(BF16).rearrange("p (h c v) -> p h c v", h=2, c=8)
        Spong = carve(256, F32, "p (h v) -> p h v", h=2)
        hg_end = _off[0]
        _off[0] = 0
        ue = carve(8 * 514, F32, "p (k t) -> p k t", k=8)
        zT = carve(2048, BF16, "p (k t) -> p k t", k=8)
        cvt = [carve(512, F32) for _ in range(2)]
        t1t = [carve(512, F32) for _ in range(2)]
        assert max(hg_end, _off[0]) <= 22 * GMAX // 2

        pbank = [st.enter_context(nc.psum_tensor("pb%d" % i, [128, 512], F32)) for i in range(8)]
        _pbi = [0]
        nrot = [8]

        def PB():
            i = _pbi[0] % nrot[0]
            _pbi[0] += 1
            return pbank[i], ("ps", i)

        mixkeys = set() if collect else set(_MIXKEYS[stage])

        def mk(*k):
            mixkeys.add(k)
            return k

        akeys = set()

        def barrier():
            keys = list(akeys | mixkeys)
            S.op("dve", lambda e: e.memset(dummy[:, 0:1], 0.0), writes=keys + ["dummy"])

        def fsz(ap):
            n = 1
            for d in ap.shape[1:]:
                n *= d
            return n

        def mm(out, lhsT, rhs, start, stop, reads, writes, skip=False):
            c = max(fsz(rhs), 48) / 2400.0 + 0.004
            if skip:
                S.op("pe", lambda e: e.matmul(out, lhsT=lhsT, rhs=rhs, start=start, stop=stop,
                                              skip_group_check=True), reads, writes, cost=c)
            else:
                S.op("pe", lambda e: e.matmul(out, lhsT=lhsT, rhs=rhs, start=start, stop=stop), reads, writes, cost=c)

        def tr(out, in_, ident, reads, writes):
            S.op("pe", lambda e: e.transpose(out=out, in_=in_, identity=ident), reads, writes, cost=0.25)

        def act(out, in_, func, reads, writes, scale=None, bias=None):
            kw = {}
            if scale is not None:
                kw["scale"] = scale
            if bias is not None:
                kw["bias"] = bias
            tset = {AF.Silu: "silu", AF.Tanh: "silu", AF.Ln: "lnexp", AF.Exp: "lnexp"}.get(func)
            S.op("act", lambda e: e.activation(out=out, in_=in_, func=func, **kw), reads, writes,
                 cost=0.2 + fsz(out) * 0.0008, tset=tset)

        def tt(out, in0, in1, op, reads, writes, eng="dve"):
            S.op(eng, lambda e: e.tensor_tensor(out=out, in0=in0, in1=in1, op=op), reads, writes,
                 cost=0.1 + fsz(out) * 0.00115)

        def ts(out, in0, s1, s2, op0, op1, reads, writes, eng="dve"):
            if op1 is None:
                S.op(eng, lambda e: e.tensor_scalar(out=out, in0=in0, scalar1=s1, scalar2=None, op0=op0),
                     reads, writes, cost=0.1 + fsz(out) * 0.0009)
            else:
                S.op(eng, lambda e: e.tensor_scalar(out=out, in0=in0, scalar1=s1, scalar2=s2, op0=op0, op1=op1),
                     reads, writes, cost=0.1 + fsz(out) * 0.0009)

        def stt(out, in0, scalar, in1, op0, op1, reads, writes):
            S.op("dve", lambda e: e.scalar_tensor_tensor(out=out, in0=in0, scalar=scalar, in1=in1,
                                                         op0=op0, op1=op1), reads, writes,
                 cost=0.12 + fsz(out) * 0.0012)

        def cp(eng, out, in_, reads, writes):
            if eng == "act":
                S.op("act", lambda e: e.copy(out=out, in_=in_), reads, writes, cost=0.18 + fsz(out) * 0.0007)
            else:
                S.op(eng, lambda e: e.tensor_copy(out=out, in_=in_), reads, writes, cost=0.1 + fsz(out) * 0.0008)

        def dma(eng, out, in_, reads, writes, dsem):
            nb = 128 * fsz(out) * (4 if out.dtype == F32 else 4)
            S.op(eng, lambda e: e.dma_start(out=out, in_=in_), reads, writes, dsem=dsem, cost=nb / 360e3)

        wsem = [S.new_dsem() for _ in range(NS)]
        _wi = [0]
        ns_eff = [NS]

        def load_w(W2d, k0, nk, c0):
            slot = _wi[0] % ns_eff[0]
            _wi[0] += 1
            Wv = W2d.rearrange("(kc p) n -> p kc n", p=128)
            dma("pool", wslot[slot][:, 0:nk, :], Wv[:, k0:k0 + nk, c0:c0 + 256], [], [("w", slot)], wsem[slot])
            return slot

        C = "const"
        d_vec = S.new_dsem()
        dma("sp", vec[:, :, :], vecs_d.rearrange("p (v k) -> p v k", v=NV), [], [C], d_vec)

        def pool(fn, writes=(C,), reads=()):
            S.op("pool", fn, reads, list(writes))

        pool(lambda e: e.memset(identf[:, :], 0.0))
        pool(lambda e: e.affine_select(out=identf[:, :], in_=identf[:, :], pattern=[[-1, 128]],
                                       compare_op=ALU.not_equal, fill=1.0, base=0, channel_multiplier=1))
        pool(lambda e: e.tensor_copy(out=identb[:, :], in_=identf[:, :]))
        pool(lambda e: e.memset(onesb[:, :], 1.0))
        pool(lambda e: e.memset(epst[:, :], EPS))
        pool(lambda e: e.memset(maskU[:, :], 1.0))
        pool(lambda e: e.affine_select(out=maskU[:, :], in_=maskU[:, :], pattern=[[1, 128]],
                                       compare_op=ALU.is_ge, fill=0.0, base=0, channel_multiplier=-1))
        pool(lambda e: e.tensor_copy(out=maskS[:, :], in_=maskU[:, :]))
        pool(lambda e: e.memset(maskU[0:64, 64:128], 0.0))
        for j in range(1, 16):
            pool(lambda e, j=j: e.affine_select(out=maskS[:, 8 * j:8 * j + 8], in_=maskS[:, 8 * j:8 * j + 8],
                                                pattern=[[0, 8]], compare_op=ALU.is_ge, fill=0.0,
                                                base=-8 * j, channel_multiplier=1))
        pool(lambda e: e.memset(rmP[:, :, :], 1.0))
        pool(lambda e: e.memset(rmP[:, :, 0:1], 0.0))
        pool(lambda e: e.memset(rmS[:, :, :], 1.0))
        pool(lambda e: e.memset(rmS[:, :, 0:1], 0.0))
        pool(lambda e: e.memset(mrow[0:64, 0:1], 1.0))
        pool(lambda e: e.memset(mrow[64:128, 0:1], 0.0))
        pool(lambda e: e.memset(mrow[0:64, 1:2], 0.0))
        pool(lambda e: e.memset(mrow[64:128, 1:2], 1.0))
        pool(lambda e: e.memset(RM[:, :], 1.0))
        pool(lambda e: e.affine_select(out=RM[:, :], in_=RM[:, :], pattern=[[-8, 16]], compare_op=ALU.is_ge,
                                       fill=0.0, base=0, channel_multiplier=1))
        pool(lambda e: e.affine_select(out=RM[:, :], in_=RM[:, :], pattern=[[8, 16]], compare_op=ALU.is_ge,
                                       fill=0.0, base=7, channel_multiplier=-1))
        pool(lambda e: e.memset(Sst[:, :, :], 0.0), writes=[("S", h) for h in range(8)])
        pool(lambda e: e.memset(Sbb[:, :, :], 0.0), writes=[("Sb", h) for h in range(8)])
        pool(lambda e: e.memset(carry[:, :, :], 0.0), writes=["carry"])
        tt(lbc[:, 0, :], vec[:, V_LB, :], vec[:, V_LB + 1, :], ALU.subtract, [C], ["lbc"])
        act(lbc[:, 5, :], lbc[:, 0, :], AF.Tanh, ["lbc"], ["lbc"], scale=0.5)
        ts(lbc[:, 0, :], lbc[:, 5, :], 0.5, 0.5, ALU.mult, ALU.add, ["lbc"], ["lbc"])
        ts(lbc[:, 1, :], lbc[:, 0, :], -0.5, 0.5, ALU.mult, ALU.add, ["lbc"], ["lbc"])
        tt(lbc[:, 2, :], lbc[:, 0, :], lbc[:, 1, :], ALU.add, ["lbc"], ["lbc"])
        ts(lbc[:, 3, :], lbc[:, 1, :], -1.0, None, ALU.mult, None, ["lbc"], ["lbc"])
        ts(lbc[:, 4, :], lbc[:, 2, :], -1.0, 1.0, ALU.mult, ALU.add, ["lbc"], [C, "lbc"])
        S.freeze(C)

        def slice_of_tile(i):
            return i // 4 if i < 16 else 4

        ys_ld = [S.new_dsem() for _ in range(2)]
        ys_st = [S.new_dsem() for _ in range(2)]
        def load_x(tiles):
          for i in tiles:
              s = slice_of_tile(i)
              sl = i % 2
              dma("sp", ystage[sl][:, :], x_d[i * 128:(i + 1) * 128, :], [], [("ys", sl)], ys_ld[sl])
              for half in range(2):
                  pb, pk = PB()
                  for q in range(4):
                      kc = half * 4 + q
                      tr(pb[:, q * 128:(q + 1) * 128], ystage[sl][:, kc * 128:(kc + 1) * 128], identf[:, :],
                         [("ys", sl), C], [pk])
                  cp("act" if half == 0 else "dve", xT[:, half * 4:half * 4 + 4, XO(i * 128):XO(i * 128) + 128],
                     pb[:, :].rearrange("p (k t) -> p k t", k=4), [pk],
                     [("x", kc, XK(s)) for kc in range(half * 4, half * 4 + 4)])


        if 0 in DBG_GROUPS:
            load_x(range(0, 8))

        def rstd_of(src_aps, src_keys, L, inv_n):
            pb, pk = PB()
            n = len(src_aps)
            for i, ap in enumerate(src_aps):
                act(sqb[i % 2][:, :L], ap, AF.Square, [src_keys[i]], [("sq", i % 2)])
                mm(pb[:, :L], onesb[:, :], sqb[i % 2][:, :L], i == 0, i == n - 1, [("sq", i % 2), C], [pk])
            act(lt[:, :L], pb[:, :L], AF.Ln, [pk, C], ["lt"], scale=inv_n, bias=epst[:, 0:1])
            act(rstd[:, :L], lt[:, :L], AF.Exp, ["lt"], ["rstd"], scale=-0.5)

        def norm_to_h(g, s, gv):
            s0, L = SLICES[s]
            off = s0 - GSTART[g]
            rstd_of([xT[:, kc, XO(s0):XO(s0) + L] for kc in range(8)], [("x", kc, XK(s)) for kc in range(8)], L, 1.0 / D)
            for kc in range(8):
                stt(hT[:, kc, off:off + L], xT[:, kc, XO(s0):XO(s0) + L], vec[:, gv, kc:kc + 1], rstd[:, :L],
                    ALU.mult, ALU.mult, [("x", kc, XK(s)), "rstd", C], [("h", kc, s)])

        def ffn(layer, which, g):
            gv = (V_NF1 if which == 0 else V_NF2) + layer
            for s in GROUPS[g]:
                norm_to_h(g, s, gv)
            Wup = w_ffn_up[which][layer]
            Wdn = w_ffn_dn[which][layer]
            cnt = 0
            for u in range(11):
                sg_ = load_w(Wup, 0, 8, u * 256)
                su_ = load_w(Wup, 0, 8, DFF + u * 256)
                for fi in range(2):
                    f = 2 * u + fi
                    for s in GROUPS[g]:
                        s0, L = SLICES[s]
                        off = s0 - GSTART[g]
                        pg, pgk = PB()
                        for kc in range(8):
                            mm(pg[:, :L], wslot[sg_][:, kc, fi * 128:(fi + 1) * 128], hT[:, kc, off:off + L],
                               kc == 0, kc == 7, [("w", sg_), ("h", kc, s)], [pgk])
                        pu, puk = PB()
                        for kc in range(8):
                            mm(pu[:, :L], wslot[su_][:, kc, fi * 128:(fi + 1) * 128], hT[:, kc, off:off + L],
                               kc == 0, kc == 7, [("w", su_), ("h", kc, s)], [puk])
                        si = cnt % 3
                        cnt += 1
                        act(sil[si][:, :L], pg[:, :L], AF.Silu, [pgk], [("sil", si)])
                        akeys.add(("a", f, s))
                        tt(aT[:, f, off:off + L], sil[si][:, :L], pu[:, :L], ALU.mult, [("sil", si), puk],
                           [("a", f, s)])
            for v in range(4):
                sl3 = [load_w(Wdn, 0, 8, v * 256), load_w(Wdn, 8, 8, v * 256), load_w(Wdn, 16, 6, v * 256)]
                for oi in range(2):
                    kco = 2 * v + oi
                    for s in GROUPS[g]:
                        s0, L = SLICES[s]
                        off = s0 - GSTART[g]
                        pb, pk = PB()
                        for kk in range(22):
                            ws = sl3[kk // 8]
                            mm(pb[:, :L], wslot[ws][:, kk % 8, oi * 128:(oi + 1) * 128], aT[:, kk, off:off + L],
                               kk == 0, kk == 21, [("w", ws), ("a", kk, s)], [pk])
                        stt(xT[:, kco, XO(s0):XO(s0) + L], pb[:, :L], 0.5, xT[:, kco, XO(s0):XO(s0) + L], ALU.mult, ALU.add,
                            [pk, ("x", kco, XK(s))], [("x", kco, XK(s))])

        s0f_ld = [S.new_dsem() for _ in range(2)]
        s0f_st = [S.new_dsem() for _ in range(2)]
        d_shp = S.new_dsem()
        Win = w_hin[0]

        def merge_gen(ga, na, gr, nr):
            ia = ir = 0
            a_done = ga is None
            r_done = gr is None
            while not (a_done and r_done):
                pick_a = (not a_done) and (r_done or ia * nr <= ir * na)
                if pick_a:
                    try:
                        next(ga)
                        ia += 1
                        yield
                    except StopIteration:
                        a_done = True
                else:
                    try:
                        next(gr)
                        ir += 1
                        yield
                    except StopIteration:
                        r_done = True

        def drain(gen):
            for _ in gen:
                pass

        kt1 = wslot[3][:, 0:4, :].rearrange("p a b -> p (a b)")
        kh1 = wslot[3][:, 4:8, :].rearrange("p a b -> p (a b)")
        qt1 = wslot[4][:, 0:4, :].rearrange("p a b -> p (a b)")

        def hviews(L, bs):
            b3, b2 = bs
            HB = 1024 // L
            NP = L // 128
            kt_, kh_, qt_ = (ktT, khT, qtT) if b2 == 0 else (kt1, kh1, qt1)
            bA_ = bufA2[b3] if b3 < 2 else bufA3[:, :]
            bB_ = bufB2[b3] if b3 < 2 else bufB3[:, :]
            vt_ = v_tok2[b3] if b3 < 2 else vt3[:, :]
            kht_ = kh_tok if b2 == 0 else kht_b[:, :]
            khtH_ = khtH if b2 == 0 else khtH_b[:, :].rearrange("p (n c) -> p n c", n=4)
            v = dict(
                bA=bA_.rearrange("p (h t) -> p h t", h=HB),
                bB=bB_.rearrange("p (h t) -> p h t", h=HB),
                kt=kt_.rearrange("p (h t) -> p h t", h=HB),
                kh=kh_.rearrange("p (h t) -> p h t", h=HB),
                qt=qt_.rearrange("p (h t) -> p h t", h=HB),
                vt=vt_.rearrange("p (n c) -> p n c", n=NP),
                kht=kht_.rearrange("p (n c) -> p n c", n=NP),
                khtH=khtH_,
                sball=SbAll if b2 == 0 else SbAll_b,
                pong=Spong if b2 == 0 else Spong_b,
            )
            return v

        def hk(name, bs, hl, L):
            b3, b2 = bs
            q = hl if L == 512 else hl // 4
            if name in ("bA", "bB"):
                return mk(name, b3, q)
            if b2 == 1:
                return ("w", 3) if name in ("kt", "kh") else ("w", 4)
            return mk(name, q)

        def genA(g, s, bi, bs, kinds=("q", "g", "i", "f")):
            s0, L = SLICES[s]
            off = s0 - GSTART[g]
            sample = (s == 4)
            HB = 1024 // L
            NP = L // 128
            hb0 = bi * HB
            V = hviews(L, bs)
            if bi == 0 and sample:
                norm_to_h(g, s, V_NMIX + 0)
                yield
            vt = vt_s[:, :].rearrange("p (n c) -> p n c", n=1) if sample else V["vt"]
            vtk = mk("vts") if sample else mk("vt", bs[0])
            gi = 0
            for kind, base in (("q", 0), ("g", 3 * D), ("i", 2 * D), ("f", D)):
                if kind not in kinds:
                    continue
                for uu in range(HB // 2):
                    ws = load_w(Win, 0, 8, base + (hb0 + 2 * uu) * 128)
                    if kind == "i":
                        for p in range(NP):
                            pb, pk = PB()
                            for kc in range(8):
                                mm(pb[:, 0:256], hT[:, kc, off + p * 128:off + (p + 1) * 128],
                                   wslot[ws][:, kc, :], kc == 0, kc == 7, [("w", ws), ("h", kc, s)], [pk])
                            cp("act" if p % 2 == 0 else "dve", vt[:, p, uu * 256:(uu + 1) * 256],
                               pb[:, 0:256], [pk], [vtk])
                            yield
                        continue
                    for hi in range(2):
                        hl = 2 * uu + hi
                        h = hb0 + hl
                        pb, pk = PB()
                        for kc in range(8):
                            mm(pb[:, :L], wslot[ws][:, kc, hi * 128:(hi + 1) * 128], hT[:, kc, off:off + L],
                               kc == 0, kc == 7, [("w", ws), ("h", kc, s)], [pk])
                        if kind == "f":
                            act(V["bA"][:, hl, :], pb[:, :L], AF.Tanh, [pk], [hk("bA", bs, hl, L)], scale=0.5)
                        elif kind == "q":
                            act(V["bB"][:, hl, :], pb[:, :L], AF.Silu, [pk], [hk("bB", bs, hl, L)])
                        else:
                            si = gi % 2
                            gi += 1
                            act(sil[si][:, :L], pb[:, :L], AF.Tanh, [pk], [("sil", si)], scale=0.5)
                            if sample:
                                ts(sg_s[:, h, :], sil[si][:, :L], 0.5, 0.5, ALU.mult, ALU.add, [("sil", si)],
                                   [("sgs", h)])
                            elif bi == 0:
                                ts(sgN[:, hl, :L], sil[si][:, :L], 0.5, 0.5, ALU.mult, ALU.add, [("sil", si)],
                                   [("sgn", hl)])
                            else:
                                ts(sgT[:, h, :L], sil[si][:, :L], 0.5, 0.5, ALU.mult, ALU.add, [("sil", si)],
                                   [mk("sg", h)])
                        yield

        def stageB(s, bs, hb0, hl, V):
            s0, L = SLICES[s]
            sample = (s == 4)
            CS = 8 if sample else 64
            NC = L // CS
            rm = (rmS[:, :, :].rearrange("p a b -> p (a b)") if sample
                  else rmP[:, :, :].rearrange("p a b -> p (a b)"))
            h = hb0 + hl
            c1 = lbc[:, 1, h:h + 1]
            c0 = lbc[:, 2, h:h + 1]
            k1 = lbc[:, 3, h:h + 1]
            k0 = lbc[:, 4, h:h + 1]
            TT = T if hl % 2 == 0 else Tb
            T1, T2, T3, T4 = (TT[0][:, :L], TT[1][:, :L], TT[2][:, :L], TT[3][:, :L])
            K = [mk("T", hl % 2, i) for i in range(4)]
            kA = hk("bA", bs, hl, L)
            kB = hk("bB", bs, hl, L)
            act(T1, V["bA"][:, hl, :], AF.Ln, [kA, C], [K[0]], scale=c1, bias=c0)
            S.op("dve", lambda e: e.tensor_tensor_scan(out=T2, data0=rm[:, :L], data1=T1, initial=0.0,
                                                       op0=ALU.mult, op1=ALU.add), [K[0], C], [K[1]],
                 cost=0.1 + L * 0.0022)
            act(T3, T2, AF.Exp, [K[1]], [K[2]])
            act(T1, T2, AF.Exp, [K[1]], [K[0]], scale=-1.0)
            b3 = T2.rearrange("p (c t) -> p c t", t=CS)
            tt(T4.rearrange("p (c t) -> p c t", t=CS),
               b3[:, :, CS - 1:CS].broadcast_to([128, NC, CS]), b3, ALU.subtract, [K[1]], [K[3]])
            act(T4, T4, AF.Exp, [K[3]], [K[3]])
            eb_t, ebk = (ebl_s, ("ebls", h)) if sample else (ebl, ("ebl", h))
            cp("dve", eb_t[:, h, 0:NC].unsqueeze(2),
               T3.rearrange("p (c t) -> p c t", t=CS)[:, :, CS - 1:CS], [K[2]], [ebk])
            act(T2, V["bA"][:, hl, :], AF.Identity, [kA, C], [K[1]], scale=k1, bias=k0)
            tt(V["kt"][:, hl, :], T2, T1, ALU.mult, [K[1], K[0]], [hk("kt", bs, hl, L)])
            tt(V["kh"][:, hl, :], T2, T4, ALU.mult, [K[1], K[3]], [hk("kh", bs, hl, L)])
            tt(V["qt"][:, hl, :], V["bB"][:, hl, :], T3, ALU.mult, [kB, K[2]], [hk("qt", bs, hl, L)])
            if sample:
                tt(qtb_s[:, h, :], V["bB"][:, hl, :], T3, ALU.mult, [kB, K[2]], [mk("q32s")])

        def headnorm(L, o_ap, o_key, h, sg_ap, sg_key, sgi_ap=None, sgi_key=None):
            if sgi_ap is None:
                sgi_ap, sgi_key = sg_ap, sg_key
            rstd_of([o_ap], [o_key], L, 1.0 / 128)
            stt(Thn[:, :L], o_ap, vec[:, V_HNORM, h:h + 1], rstd[:, :L], ALU.mult, ALU.mult,
                [o_key, "rstd", C], ["Thn"])
            tt(sg_ap, Thn[:, :L], sgi_ap, ALU.mult, ["Thn", sgi_key], [sg_key])

        def outproj(s, sg_of, sgkey_of):
            s0, L = SLICES[s]
            for v in range(4):
                ws = load_w(w_hout[0], 0, 8, v * 256)
                for oi in range(2):
                    kco = 2 * v + oi
                    pb, pk = PB()
                    for h in range(8):
                        mm(pb[:, :L], wslot[ws][:, h, oi * 128:(oi + 1) * 128], sg_of(h), h == 0, h == 7,
                           [("w", ws), sgkey_of(h)], [pk])
                    tt(xT[:, kco, XO(s0):XO(s0) + L], pb[:, :L], xT[:, kco, XO(s0):XO(s0) + L], ALU.add, [pk, ("x", kco, XK(s))],
                       [("x", kco, XK(s))])
                    yield

        khtH = Sbb[:, :, :].rearrange("p a b -> p (a b)").rearrange("p (n c) -> p n c", n=4)

        def RBh(s, bi, bs, hl):
            stageB(s, bs, bi * 2, hl, hviews(512, bs))

        def RT(s, bi, bs):
            L = 512
            V = hviews(L, bs)
            for p in range(4):
                pb, pk = PB()
                for q in range(2):
                    mm(pb[:, q * 128:(q + 1) * 128], V["kh"][:, q, p * 128:(p + 1) * 128], identb[:, :],
                       True, True, [hk("kh", bs, q, L), C], [pk])
                cp("act", V["kht"][0:64, p, 0:256], pb[0:64, 0:256], [pk], [mk("kht", bs[1])])
                cp("dve", V["khtH"][64:128, p, 0:256], pb[64:128, 0:256], [pk], [mk("khtH", bs[1])])
            for hl in range(2):
                for c in range(8):
                    p = c // 2
                    bk = 4 + hl * 2 + c // 4
                    kk_ = V["kht"] if c % 2 == 0 else V["khtH"]
                    mm(pbank[bk][:, (c % 4) * 128:(c % 4 + 1) * 128], kk_[:, p, hl * 128:(hl + 1) * 128],
                       V["vt"][:, p, hl * 128:(hl + 1) * 128], True, True, [mk("kht", bs[1]), mk("khtH", bs[1]), mk("vt", bs[0])],
                       [("ps", bk)])

        def RChain(s, bi, bs):
            hb0 = bi * 2
            V = hviews(512, bs)
            b2 = bs[1]
            for c in range(8):
                for hl in range(2):
                    h = hb0 + hl
                    bk = 4 + hl * 2 + c // 4
                    P_ = pbank[bk][:, (c % 4) * 128:(c % 4 + 1) * 128]
                    e_ap = ebl[:, h, c:c + 1]
                    pong, pkey = V["pong"][:, hl, :], mk("pong", b2, hl)
                    if c % 2 == 0:
                        src, skey, dst, dkey = Sst[:, h, :], ("S", h), pong, pkey
                    else:
                        src, skey, dst, dkey = pong, pkey, Sst[:, h, :], ("S", h)
                    cp("act", V["sball"][:, hl, c, :], src, [skey], [mk("sball", b2, hl, c)])
                    stt(dst, src, e_ap, P_, ALU.mult, ALU.add, [skey, ("ebl", h), ("ps", bk)], [dkey])

        def RO(s, bi, bs):
            L = 512
            V = hviews(L, bs)
            ami = 0
            for p in range(4):
                slots = {}
                for hl in range(2):
                    pa, pak = PB()
                    mm(pa[:, 0:128], V["kt"][:, hl, p * 128:(p + 1) * 128], V["qt"][:, hl, p * 128:(p + 1) * 128],
                       True, True, [hk("kt", bs, hl, L), hk("qt", bs, hl, L)], [pak])
                    a_ = ami % 4
                    ami += 1
                    slots[hl] = a_
                    tt(am[a_][:, :], pa[:, 0:128], maskU[:, :], ALU.mult, [pak, C], [("am", a_)])
                for hl in range(2):
                    a_ = slots[hl]
                    po, pok = PB()
                    mm(po[:, 0:128], V["vt"][:, p, hl * 128:(hl + 1) * 128], am[a_][:, :], True, False,
                       [mk("vt", bs[0]), ("am", a_)], [pok])
                    for c in range(2):
                        r0 = c * 64
                        mm(po[:, r0:r0 + 64], V["sball"][:, hl, 2 * p + c, :],
                           V["qt"][:, hl, p * 128 + r0:p * 128 + r0 + 64], False, c == 1,
                           [mk("sball", bs[1], hl, 2 * p + c), hk("qt", bs, hl, L)], [pok])
                    cp("act", V["bA"][:, hl, p * 128:(p + 1) * 128], po[:, 0:128], [pok], [hk("bA", bs, hl, L)])

        def RH(g, s, bi, bs):
            L = 512
            V = hviews(L, bs)
            for hl in range(2):
                h = bi * 2 + hl
                if bi == 0:
                    headnorm(L, V["bA"][:, hl, :], hk("bA", bs, hl, L), h, sgT[:, h, :L], mk("sg", h),
                             sgN[:, hl, :L], ("sgn", hl))
                else:
                    headnorm(L, V["bA"][:, hl, :], hk("bA", bs, hl, L), h, sgT[:, h, :L], mk("sg", h))
            if bi == 3:
                drain(outproj(s, lambda h: sgT[:, h, :L], lambda h: mk("sg", h)))
                if s == 3:
                    dma("sp", shp_d.rearrange("h k v -> k h v"), Sst[:, :, :], [("S", h) for h in range(8)], [],
                        d_shp)

        def take(gen, n):
            if gen is None:
                return
            for _ in range(n):
                try:
                    next(gen)
                except StopIteration:
                    return

        def prompt_pipeline(g, slices):
            batches = [(s, bi) for s in slices for bi in range(4)]
            n = len(batches)
            S.op("dve", lambda e: e.memset(kh_tok[64:128, :], 0.0), [], [mk("kht", 0)])
            S.op("dve", lambda e: e.memset(khtH[0:64, :, :], 0.0), [], [mk("khtH", 0)])
            S.op("dve", lambda e: e.memset(kht_b[64:128, :], 0.0), [], [mk("kht", 1)])
            S.op("dve", lambda e: e.memset(khtH_b[0:64, :], 0.0), [], [mk("khtH", 1)])
            norm_to_h(g, slices[0], V_NMIX + 0)

            def A(i, kinds):
                if i < n:
                    drain(genA(g, batches[i][0], batches[i][1], (i % 3, i % 2), kinds))

            def RB_(i, hl):
                if i < n:
                    RBh(batches[i][0], batches[i][1], (i % 3, i % 2), hl)

            A(0, ("q", "g", "i", "f"))
            RB_(0, 0)
            RB_(0, 1)
            A(1, ("q", "g", "i", "f"))
            for i, (s, bi) in enumerate(batches):
                bs = (i % 3, i % 2)
                RT(s, bi, bs)
                yield
                RChain(s, bi, bs)
                yield
                RB_(i + 1, 0)
                if bi == 1 and s != slices[-1]:
                    norm_to_h(g, slices[slices.index(s) + 1], V_NMIX + 0)
                yield
                late_g = (bi == 3)
                A(i + 2, ("q",) if late_g else ("q", "g"))
                yield
                RO(s, bi, bs)
                yield
                RB_(i + 1, 1)
                yield
                A(i + 2, ("i",))
                yield
                RH(g, s, bi, bs)
                yield
                if late_g:
                    A(i + 2, ("g",))
                A(i + 2, ("f",))
                yield

        def sample_front(g):
            s = 4
            L = 128
            V = hviews(L, SSET)
            drain(genA(g, s, 0, SSET))
            for h in range(8):
                stageB(s, SSET, 0, h, V)
            for h0 in (0, 4):
                pb, pk = PB()
                for q in range(4):
                    mm(pb[:, q * 128:(q + 1) * 128], V["kh"][:, h0 + q, :], identb[:, :], True, True,
                       [hk("kh", SSET, h0 + q, L), C], [pk])
                cp("act", kht_s[:, h0 * 128:(h0 + 4) * 128], pb[:, 0:512], [pk], [mk("khts")])
            for h in range(8):
                pa, pak = PB()
                mm(pa[:, 0:128], V["kt"][:, h, :], V["qt"][:, h, :], True, True,
                   [hk("kt", SSET, h, L), hk("qt", SSET, h, L)], [pak])
                a_ = h % 4
                tt(am[a_][:, :], pa[:, 0:128], maskS[:, :], ALU.mult, [pak, C], [("am", a_)])
                po, pok = pbank[6 + h // 4], ("ps", 6 + h // 4)
                mm(po[:, (h % 4) * 128:(h % 4 + 1) * 128], vt_s[:, h * 128:(h + 1) * 128], am[a_][:, :],
                   True, True, [mk("vts"), ("am", a_)], [pok])
                cp("act", ynT[:, h, :], po[:, (h % 4) * 128:(h % 4 + 1) * 128], [pok], ["os"])

        o_s = ynT

        def sample_back():
            s = 4
            L = 128

            def load(j):
                sl = j % 2
                dma("sp", ystage[sl][:, :].rearrange("p (h v) -> p h v", h=8), sh_d[j].rearrange("h k v -> k h v"),
                    [], [("ys", sl)], s0f_ld[sl])

            load(0)
            load(1)
            yield
            for j in range(16):
                sl = j % 2
                ts(vmask[sl][:, :], vt_s[:, :], RM[:, j:j + 1], None, ALU.mult, None, [mk("vts"), C], [mk("vm", 0)])
                cp("act", S0bt[sl][:, :], ystage[sl][:, :], [("ys", sl)], [("S0b", 0)])
                pi, pik = PB()
                for h in range(8):
                    mm(pi[:, h * 8:(h + 1) * 8], S0bt[sl][:, h * 128:(h + 1) * 128], qtb_s[:, h, 8 * j:8 * j + 8],
                       True, True, [("S0b", 0), mk("q32s")], [pik])
                tt(o_s[:, :, 8 * j:8 * j + 8], o_s[:, :, 8 * j:8 * j + 8],
                   pi[:, 0:64].rearrange("p (h t) -> p h t", t=8), ALU.add, ["os", pik], ["os"])
                for h in range(8):
                    S0 = ystage[sl][:, h * 128:(h + 1) * 128]
                    ps_, psk = PB()
                    mm(ps_[:, 0:128], kht_s[:, h * 128:(h + 1) * 128], vmask[sl][:, h * 128:(h + 1) * 128],
                       True, True, [mk("khts"), mk("vm", 0)], [psk])
                    stt(S0, S0, ebl_s[:, h, j:j + 1], ps_[:, 0:128], ALU.mult, ALU.add,
                        [("ys", sl), ("ebls", h), psk], [("ys", sl)])
                dma("sp", shs_d[j].rearrange("h k v -> k h v"),
                    ystage[sl][:, :].rearrange("p (h v) -> p h v", h=8), [("ys", sl)], [], s0f_st[sl])
                if j + 2 < 16:
                    load(j + 2)
                yield
            for h in range(8):
                headnorm(L, o_s[:, h, :], "os", h, sg_s[:, h, :], ("sgs", h))
            yield from outproj(s, lambda h: sg_s[:, h, :], lambda h: ("sgs", h))

        def hgrn(g):
            ns_eff[0] = 3
            nrot[0] = 4
            hgrn_(g)
            nrot[0] = 8
            ns_eff[0] = NS

        def hgrn_(g):
            slices = [s for s in GROUPS[g] if s != 4]
            pp = prompt_pipeline(g, slices)
            if 4 in GROUPS[g]:
                sample_front(g)
                npp = len(slices) * 4 * (11 + 30)
                if SMODE == 2:
                    drain(sample_back())
                    drain(pp)
                else:
                    drain(merge_gen(pp, 72, sample_back(), 26))
            else:
                drain(pp)

        d_sc = S.new_dsem()
        d_scs = S.new_dsem()
        d_scp = S.new_dsem()

        def conv(g):
            Win = w_cin[0]
            for s in GROUPS[g]:
                s0, L = SLICES[s]
                off = s0 - GSTART[g]
                sample = (s == 4)
                norm_to_h(g, s, V_NMIX + 1)

                def uview(kc, a, b):
                    if sample:
                        return ue[:, kc, 0:160].rearrange("p (j t) -> p j t", t=10)[:, :, a:a + 8]
                    return ue[:, kc, a:a + L]

                def v3(ap):
                    return ap.rearrange("p (j t) -> p j t", t=8) if sample else ap

                if sample:
                    dma("sp", ystage[0][0:32, :], sc_d[:, :], [], [("ys", 0)], d_sc)
                    pb, pk = PB()
                    for kc in range(8):
                        tr(pb[:, kc * 32:(kc + 1) * 32], ystage[0][0:32, kc * 128:(kc + 1) * 128], identf[0:32, 0:32],
                           [("ys", 0), C], [pk])
                    for kc in range(8):
                        cp("dve", ue[:, kc, 0:160].rearrange("p (j t) -> p j t", t=10)[:, :, 0:2],
                           pb[:, kc * 32:(kc + 1) * 32].rearrange("p (j t) -> p j t", t=2), [pk], [mk("ue", kc)])
                else:
                    for kc in range(8):
                        cp("dve", ue[:, kc, 0:2], carry[:, kc, :], ["carry"], [mk("ue", kc)])
                ci = 0
                for m in range(4):
                    wc = load_w(Win, 0, 8, D + m * 256)
                    wv = load_w(Win, 0, 8, 2 * D + m * 256)
                    wb = load_w(Win, 0, 8, m * 256)
                    for fi in range(2):
                        kc = 2 * m + fi
                        pc, pck = PB()
                        for k2 in range(8):
                            mm(pc[:, :L], wslot[wc][:, k2, fi * 128:(fi + 1) * 128], hT[:, k2, off:off + L],
                               k2 == 0, k2 == 7, [("w", wc), ("h", k2, s)], [pck])
                        pv, pvk = PB()
                        for k2 in range(8):
                            mm(pv[:, :L], wslot[wv][:, k2, fi * 128:(fi + 1) * 128], hT[:, k2, off:off + L],
                               k2 == 0, k2 == 7, [("w", wv), ("h", k2, s)], [pvk])
                        pb, pbk = PB()
                        for k2 in range(8):
                            mm(pb[:, :L], wslot[wb][:, k2, fi * 128:(fi + 1) * 128], hT[:, k2, off:off + L],
                               k2 == 0, k2 == 7, [("w", wb), ("h", k2, s)], [pbk])
                        i2 = ci % 2
                        ci += 1
                        cp("act", cvt[i2][:, :L], pc[:, :L], [pck], [mk("cv", i2)])
                        tt(uview(kc, 2, 0), v3(cvt[i2][:, :L]), v3(pv[:, :L]), ALU.mult, [mk("cv", i2), pvk],
                           [mk("ue", kc)])
                        t1 = v3(t1t[i2][:, :L])
                        ts(t1, uview(kc, 0, 0), vec[:, V_CW + 0, kc:kc + 1], None, ALU.mult, None,
                           [mk("ue", kc), C], [mk("t1", i2)])
                        stt(t1, uview(kc, 1, 0), vec[:, V_CW + 1, kc:kc + 1], t1, ALU.mult, ALU.add,
                            [mk("ue", kc), mk("t1", i2), C], [mk("t1", i2)])
                        stt(t1, uview(kc, 2, 0), vec[:, V_CW + 2, kc:kc + 1], t1, ALU.mult, ALU.add,
                            [mk("ue", kc), mk("t1", i2), C], [mk("t1", i2)])
                        tt(zT[:, kc, :L], t1t[i2][:, :L], pb[:, :L], ALU.mult, [mk("t1", i2), pbk], [mk("z", kc)])
                if sample:
                    for kc in range(8):
                        cp("dve", cso[:, kc, :].rearrange("p (j t) -> p j t", t=2),
                           ue[:, kc, 0:160].rearrange("p (j t) -> p j t", t=10)[:, :, 8:10], [mk("ue", kc)], ["cso"])
                    for half in range(2):
                        pb, pk = PB()
                        for q in range(4):
                            kc = half * 4 + q
                            tr(pb[0:32, q * 128:(q + 1) * 128], cso[:, kc, :], identf[:, :], ["cso", C], [pk])
                        cp("dve", ystage[1][0:32, half * 512:(half + 1) * 512], pb[0:32, :], [pk], [("ys", 1)])
                    dma("sp", scs_d[:, :], ystage[1][0:32, :], [("ys", 1)], [], d_scs)
                else:
                    for kc in range(8):
                        cp("dve", carry[:, kc, :], ue[:, kc, L:L + 2], [mk("ue", kc)], ["carry"])
                    if s == 3:
                        for half in range(2):
                            pb, pk = PB()
                            for q in range(4):
                                kc = half * 4 + q
                                tr(pb[0:2, q * 128:(q + 1) * 128], carry[:, kc, :], identf[:, :], ["carry", C], [pk])
                            cp("dve", ystage[1][0:2, half * 512:(half + 1) * 512], pb[0:2, :], [pk], [("ys", 1)])
                        dma("sp", scp_d[:, :], ystage[1][0:2, :], [("ys", 1)], [], d_scp)
                for v in range(4):
                    ws = load_w(w_cout[0], 0, 8, v * 256)
                    for oi in range(2):
                        kco = 2 * v + oi
                        pb, pk = PB()
                        for k2 in range(8):
                            mm(pb[:, :L], wslot[ws][:, k2, oi * 128:(oi + 1) * 128], zT[:, k2, :L], k2 == 0, k2 == 7,
                               [("w", ws), mk("z", k2)], [pk])
                        tt(xT[:, kco, XO(s0):XO(s0) + L], pb[:, :L], xT[:, kco, XO(s0):XO(s0) + L], ALU.add, [pk, ("x", kco, XK(s))],
                           [("x", kco, XK(s))])

        def final(g):
            for s in GROUPS[g]:
                s0, L = SLICES[s]
                rstd_of([xT[:, kc, XO(s0):XO(s0) + L] for kc in range(8)], [("x", kc, XK(s)) for kc in range(8)], L, 1.0 / D)
                for p in range(L // 128):
                    t0 = s0 + p * 128
                    for kc in range(8):
                        stt(ynT[:, kc, :], xT[:, kc, XO(t0):XO(t0) + 128], vec[:, V_NFIN, kc:kc + 1],
                            rstd[:, p * 128:(p + 1) * 128], ALU.mult, ALU.mult, [("x", kc, XK(s)), "rstd", C],
                            [("yn", kc), "os"])
                    sl = (t0 // 128) % 2
                    for half in range(2):
                        pb, pk = PB()
                        for q in range(4):
                            kc = half * 4 + q
                            tr(pb[:, q * 128:(q + 1) * 128], ynT[:, kc, :], identf[:, :], [("yn", kc), "os", C], [pk])
                        cp("act" if half == 0 else "dve", ystage[sl][:, half * 512:(half + 1) * 512], pb[:, :],
                           [pk], [("ys", sl)])
                    dma("sp", y_d[t0:t0 + 128, :], ystage[sl][:, :], [("ys", sl)], [], ys_st[sl])

        def mark(name):
            S.marks.append((name, len(S.ops)))

        mark("load")
        for g in DBG_GROUPS:
            if g == 1:
                load_x(range(8, 17))
            if stage >= 1:
                ffn(0, 0, g)
                mark("ffn1.L0.g%d" % g)
            if stage >= 2:
                barrier()
                hgrn(g)
                barrier()
                mark("hgrn.g%d" % g)
            if stage >= 3:
                ffn(0, 1, g)
                mark("ffn2.L0.g%d" % g)
            if stage >= 4:
                ffn(1, 0, g)
                mark("ffn1.L1.g%d" % g)
            if stage >= 5:
                barrier()
                conv(g)
                barrier()
                mark("conv.g%d" % g)
            if stage >= 6:
                ffn(1, 1, g)
                mark("ffn2.L1.g%d" % g)
            final(g)
            mark("final.g%d" % g)
        if collect:
            return set(mixkeys)
        if REORDER:
            est = S.reorder()
            print("[sched] estimated span us:", round(est, 1))
        counts = S.lower(st)
    return nc, counts


_CACHE = {}


def kernel(x_prompt, x_sample, state_hgrn, state_conv, norm_ffn1, w_ffn1_up, w_ffn1_down,
           norm_mix, norm_ffn2, w_ffn2_up, w_ffn2_down, w_hgrn_in, hgrn_lower_bounds,
           hgrn_norm, w_hgrn_out, w_conv_in, conv_w, w_conv_out, norm_final):
    f = lambda a: np.ascontiguousarray(np.asarray(a, dtype=np.float32))
    x_prompt, x_sample, state_hgrn, state_conv = f(x_prompt), f(x_sample), f(state_hgrn), f(state_conv)
    if "nc" not in _CACHE:
        _CACHE["nc"] = build_program()[0]
    nc = _CACHE["nc"]
    vlist = [f(norm_ffn1)[0], f(norm_ffn1)[1], f(norm_mix)[0], f(norm_mix)[1], f(norm_ffn2)[0], f(norm_ffn2)[1],
             f(norm_final), f(hgrn_norm)[0], f(conv_w)[0, 0], f(conv_w)[0, 1], f(conv_w)[0, 2],
             f(hgrn_lower_bounds)[0], f(hgrn_lower_bounds)[1]]
    vecs = np.stack(vlist).reshape(NV, 8, 128).transpose(2, 0, 1).reshape(128, NV * 8)
    vecs = np.ascontiguousarray(vecs)
    shared = {
        "vecs": vecs,
        "w_ffn1_up": f(w_ffn1_up), "w_ffn2_up": f(w_ffn2_up),
        "w_ffn1_down": f(w_ffn1_down), "w_ffn2_down": f(w_ffn2_down),
        "w_hgrn_in": f(w_hgrn_in), "w_hgrn_out": f(w_hgrn_out),
        "w_conv_in": f(w_conv_in), "w_conv_out": f(w_conv_out),
    }
    in_maps = []
    for c in range(NCORES):
        m = dict(shared)
        m["x"] = np.ascontiguousarray(np.concatenate(
            [x_prompt[c], x_sample[16 * c:16 * c + 16].reshape(128, D)], axis=0))
        m["sh"] = np.ascontiguousarray(state_hgrn[0, 16 * c:16 * c + 16])
        m["sc"] = np.ascontiguousarray(state_conv[0, 16 * c:16 * c + 16].reshape(32, D))
        in_maps.append(m)
    res = run_bass_kernel_spmd(nc, in_maps, core_ids=list(range(NCORES)))
    R = res.results
    y_prompt = np.stack([R[c]["y"][0:2048] for c in range(NCORES)])
    y_sample = np.concatenate([R[c]["y"][2048:].reshape(16, 8, D) for c in range(NCORES)], axis=0)
    shp = np.stack([R[c]["shp"] for c in range(NCORES)])[None]
    shs = np.concatenate([R[c]["shs"] for c in range(NCORES)], axis=0)[None]
    scp = np.stack([R[c]["scp"] for c in range(NCORES)])[None]
    scs = np.concatenate([R[c]["scs"].reshape(16, 2, D) for c in range(NCORES)], axis=0)[None]
    return (y_prompt.astype(np.float32), y_sample.astype(np.float32), shp.astype(np.float32),
            shs.astype(np.float32), scp.astype(np.float32), scs.astype(np.float32))
```
